# Optimizing a Trainium2 kernel written in Bass

```python
import jax, jax.numpy as jnp
from jax import lax
import numpy as np

D_MODEL = 4096
BATCH = 4
SEQ = 4096
DEPTH = 4

N_A = DEPTH // 2
N_B = DEPTH - N_A
ALPHA = (2.0 * DEPTH) ** 0.25
BETA = (8.0 * DEPTH) ** -0.25
CHUNK = 128
D_INNER_A = D_MODEL
GROUPS_A = D_INNER_A // 128
HEAD_DIM = 64
N_Q = D_MODEL // HEAD_DIM
N_KV = 8
GQA = N_Q // N_KV
WINDOW = 128
BLK = WINDOW
D_FF = 11008
CONV_W = 3
LN_EPS = 1e-5

kernel_name = "yoco_gmlp_swa_sink_convffn_deepnorm"


def layer_norm(x, g, b):
    xf = x.astype(jnp.float32)
    mu = jnp.mean(xf, axis=-1, keepdims=True)
    xc = xf - mu
    var = jnp.mean(xc * xc, axis=-1, keepdims=True)
    return (xc * lax.rsqrt(var + LN_EPS) * g.astype(jnp.float32) + b.astype(jnp.float32)).astype(x.dtype)


def conv_ffn(h, w_up, conv_w, conv_b, w_down):
    z = h @ w_up
    zp = jnp.pad(z, ((0, 0), (CONV_W - 1, 0), (0, 0)))
    z = conv_w[0] * zp[:, :-2] + conv_w[1] * zp[:, 1:-1] + conv_w[2] * zp[:, 2:] + conv_b
    g, u = jnp.split(z, 2, axis=-1)
    return (jax.nn.silu(g) * u) @ w_down


def gmlp_mixer(h, w_in, v_g, v_b, w_s, b_s, w_out):
    bsz, seq, _ = h.shape
    z = jax.nn.gelu(h @ w_in, approximate=False)
    u, v = jnp.split(z, 2, axis=-1)
    v = layer_norm(v, v_g, v_b)
    v = v.reshape(bsz, seq // CHUNK, CHUNK, GROUPS_A, D_INNER_A // GROUPS_A)
    causal = jnp.tril(jnp.ones((CHUNK, CHUNK), dtype=bool))
    w = jnp.where(causal[None], w_s, jnp.zeros((), w_s.dtype))
    vm = jnp.einsum('gts,bnsgc->bntgc', w, v) + b_s.T[None, None, :, :, None]
    return (u * vm.reshape(bsz, seq, D_INNER_A)) @ w_out


def shared_kv(h, w_kv):
    bsz, seq, _ = h.shape
    nb = seq // BLK
    kv = (h @ w_kv).reshape(bsz, seq, 2, N_KV, HEAD_DIM)

    def to_band(t):
        tb = t.reshape(bsz, nb, BLK, N_KV, HEAD_DIM)
        prev = jnp.pad(tb[:, :-1], ((0, 0), (1, 0), (0, 0), (0, 0), (0, 0)))
        band = jnp.concatenate([prev, tb], axis=2)
        return jnp.moveaxis(band, 1, 0)

    return to_band(kv[:, :, 0]), to_band(kv[:, :, 1])


def swa_sink_mixer(h, w_q, sinks, w_out, k_band, v_band):
    bsz, seq, _ = h.shape
    nb = seq // BLK
    scale = HEAD_DIM ** -0.5
    q = (h @ w_q).reshape(bsz, nb, BLK, N_KV, GQA, HEAD_DIM) * scale
    q = jnp.moveaxis(q, 1, 0)
    sink = sinks.astype(jnp.float32).reshape(N_KV, GQA)[:, :, None, None]
    qi = jnp.arange(BLK)[:, None]
    kj = jnp.arange(2 * BLK)[None, :]
    band_mask = (kj > qi) & (kj <= qi + WINDOW)

    def attend(args):
        qb, kb, vb, n = args
        s = jnp.einsum('bqhgd,bkhd->bhgqk', qb, kb).astype(jnp.float32)
        valid = band_mask & ((n > 0) | (kj >= BLK))
        s = jnp.where(valid, s, -jnp.inf)
        m = jnp.maximum(jnp.max(s, axis=-1, keepdims=True), sink)
        p = jnp.exp(s - m)
        denom = jnp.sum(p, axis=-1, keepdims=True) + jnp.exp(sink - m)
        return jnp.einsum('bhgqk,bkhd->bqhgd', (p / denom).astype(vb.dtype), vb)

    o = lax.map(attend, (q, k_band, v_band, jnp.arange(nb)))
    o = jnp.moveaxis(o, 0, 1).reshape(bsz, seq, N_Q * HEAD_DIM)
    return o @ w_out


def setup_inputs(seed: int = 0) -> dict:
    key = jax.random.key(seed)
    ks = jax.random.split(key, 17)

    def nrm(k, shape, scale):
        return jax.random.normal(k, shape, jnp.float32) * scale

    return {
        "x": nrm(ks[0], (BATCH, SEQ, D_MODEL), 1.0),
        "mix_in_a": nrm(ks[1], (N_A, D_MODEL, 2 * D_INNER_A), D_MODEL ** -0.5),
        "norm_v_a_g": 1.0 + nrm(ks[2], (N_A, D_INNER_A), 0.02),
        "norm_v_a_b": nrm(ks[3], (N_A, D_INNER_A), 0.02),
        "sgu_w": nrm(ks[4], (N_A, GROUPS_A, CHUNK, CHUNK), CHUNK ** -0.5),
        "sgu_b": 1.0 + nrm(ks[5], (N_A, GROUPS_A, CHUNK), 0.02),
        "mix_out_a": nrm(ks[6], (N_A, D_INNER_A, D_MODEL), D_INNER_A ** -0.5 * BETA),
        "w_kv": nrm(ks[7], (D_MODEL, 2 * N_KV * HEAD_DIM), D_MODEL ** -0.5),
        "mix_in_b": nrm(ks[8], (N_B, D_MODEL, N_Q * HEAD_DIM), D_MODEL ** -0.5),
        "sinks": nrm(ks[9], (N_B, N_Q), 0.5),
        "mix_out_b": nrm(ks[10], (N_B, N_Q * HEAD_DIM, D_MODEL), (N_Q * HEAD_DIM) ** -0.5 * BETA),
        "ffn_up": nrm(ks[11], (DEPTH, D_MODEL, 2 * D_FF), D_MODEL ** -0.5),
        "ffn_conv_w": nrm(ks[12], (DEPTH, CONV_W, 2 * D_FF), CONV_W ** -0.5),
        "ffn_conv_b": nrm(ks[13], (DEPTH, 2 * D_FF), 0.01),
        "ffn_down": nrm(ks[14], (DEPTH, D_FF, D_MODEL), D_FF ** -0.5 * BETA),
        "ln_g": 1.0 + nrm(ks[15], (DEPTH, 2, D_MODEL), 0.02),
        "ln_b": nrm(ks[16], (DEPTH, 2, D_MODEL), 0.02),
    }


def reference(x, mix_in_a, norm_v_a_g, norm_v_a_b, sgu_w, sgu_b, mix_out_a, w_kv,
              mix_in_b, sinks, mix_out_b, ffn_up, ffn_conv_w, ffn_conv_b, ffn_down,
              ln_g, ln_b):
    h = x
    k_band = v_band = None
    for l in range(DEPTH):
        if l < N_A:
            mix = gmlp_mixer(h, mix_in_a[l], norm_v_a_g[l], norm_v_a_b[l],
                             sgu_w[l], sgu_b[l], mix_out_a[l])
        else:
            if l == N_A:
                k_band, v_band = shared_kv(h, w_kv)
            j = l - N_A
            mix = swa_sink_mixer(h, mix_in_b[j], sinks[j], mix_out_b[j], k_band, v_band)
        h = layer_norm(ALPHA * h + mix, ln_g[l, 0], ln_b[l, 0])
        h = layer_norm(ALPHA * h + conv_ffn(h, ffn_up[l], ffn_conv_w[l], ffn_conv_b[l], ffn_down[l]),
                       ln_g[l, 1], ln_b[l, 1])
    return h
```

```python
import math
from contextlib import ExitStack
import numpy as np
import concourse.bass as bass
import concourse.mybir as mybir
from concourse.bass_utils import run_bass_kernel_spmd

F32 = mybir.dt.float32
BF16 = mybir.dt.bfloat16
AF = mybir.ActivationFunctionType
ALU = mybir.AluOpType
AX = mybir.AxisListType

ENGS = ("pe", "act", "dve", "pool", "sp")


class Cfg:
    def __init__(self, D, DFF, SEQ, NBC, NCORES):
        self.D = D; self.DI = D; self.DFF = DFF; self.SEQ = SEQ; self.NBC = NBC; self.NCORES = NCORES
        self.DEPTH = 4; self.N_A = 2; self.N_B = 2
        self.KC = D // 128; self.G = self.DI // 128; self.NJ = DFF // 128; self.NJ2 = 2 * self.NJ
        self.NQ = D // 64; self.NKV = self.NQ // 8; self.KVD = self.NKV * 64
        self.T = 256; self.TC = 2; self.NT = SEQ // self.T
        self.FG = 4 if self.NJ >= 8 else 2
        self.ALPHA = (2.0 * self.DEPTH) ** 0.25
        self.EPS = 1e-5


class Prog:
    def __init__(self):
        self.ops = {e: [] for e in ENGS}
        self.cnt = {}
        self.waited = {e: {} for e in ENGS}
        self.last = {e: None for e in ENGS}
        self.pending = {e: [] for e in ENGS}

    def emit(self, eng, fn, deps=(), sig=None, inc=1, serial=False):
        waits = {}
        alld = list(deps) + self.pending[eng]
        self.pending[eng] = []
        if serial and self.last[eng] is not None:
            alld.append(self.last[eng])
        for d in alld:
            if d is None:
                continue
            k, v = d
            if v > waits.get(k, 0):
                waits[k] = v
        wl = []
        wd = self.waited[eng]
        for k, v in waits.items():
            if v > wd.get(k, 0):
                wd[k] = v
                wl.append((k, v))
        tok = None
        if sig is not None:
            self.cnt[sig] = self.cnt.get(sig, 0) + inc
            tok = (sig, self.cnt[sig])
            self.last[eng] = tok
        self.ops[eng].append((wl, fn, sig, inc))
        return tok

    def barrier(self):
        toks = [self.last[e] for e in ("pe", "act", "dve")]
        for e in ("pe", "act", "dve"):
            self.pending[e] += toks


def build(cfg):
    c = cfg
    D, DI, DFF, SEQ, NBC = c.D, c.DI, c.DFF, c.SEQ, c.NBC
    KC, G, NJ, NJ2, NQ, NKV, KVD = c.KC, c.G, c.NJ, c.NJ2, c.NQ, c.NKV, c.KVD
    T, TC, NT, DEPTH, N_A, N_B = c.T, c.TC, c.NT, c.DEPTH, c.N_A, c.N_B
    ALPHA, EPS = c.ALPHA, c.EPS
    VB = min(128, KVD)
    NVB = KVD // VB
    gsz = [NJ // c.FG + (1 if i < NJ % c.FG else 0) for i in range(c.FG)]
    groups = []
    s0 = 0
    for g_ in gsz:
        groups.append((s0, s0 + g_)); s0 += g_
    KW = max(KC, G, max(gsz), 32 if NQ >= 32 else NQ)
    NW = 3

    nc = bass.Bass("TRN2", target_bir_lowering=False)

    def din(name, shape):
        return nc.dram_tensor(name, list(shape), F32, kind="ExternalInput").ap()

    NTILES = NBC * NT
    xT = din("xT", [NTILES * 128, KC * T])
    flg_d = din("flg", [128, 2 * NTILES])
    HB = 32 if NQ >= 32 else NQ
    NPARTS = NQ // HB
    w_in_a = din("w_in_a", [N_A * 2 * G * 128, KC * 128])
    w_out_a = din("w_out_a", [N_A * KC * 128, G * 128])
    w_k = din("w_k", [NKV * 128, KC * 64])
    w_v = din("w_v", [NVB * 128, KC * VB])
    w_in_b = din("w_in_b", [N_B * NQ * 128, KC * 64])
    w_out_b = din("w_out_b", [N_B * KC * NPARTS * 64, HB * 128])
    w_up = din("w_up", [DEPTH * NJ2 * 128, KC * 128])
    w_down = din("w_down", [DEPTH * KC * 128, NJ * 128])
    lnp_d = din("lnp", [128, DEPTH * 4 * KC])
    convp_d = din("convp", [DEPTH * 128, 4 * NJ2])
    sink_d = din("sinkb", [128, N_B * NQ])
    vgb_d = din("vgb", [N_A * 2 * 128, DI])
    sgub_d = din("sgub", [N_A * 128, G * 128])
    sguw_d = din("sguwT", [N_A * 128, G * 128])
    c32_d = din("cst32", [128, 640])
    cb_d = din("cstb", [128, 256])
    yT = nc.dram_tensor("yT", [NTILES * 128, KC * T], F32, kind="ExternalOutput").ap()

    P0 = Prog()
    P = Prog()
    ctx = {}
    es = ExitStack()
    sb = lambda n, s, d: es.enter_context(nc.sbuf_tensor(n, list(s), d))
    h = sb("h", [128, KC, T], F32)
    hb = sb("hb", [128, KC, T], BF16)
    kT = sb("kT", [128, NKV, 128 + T], BF16)
    Vt = sb("Vt", [128, 1 + TC, KVD], BF16)
    wring = [sb(f"wr{i}", [128, KW, 128], BF16) for i in range(NW)]
    c32 = sb("c32", [128, 640], F32)
    cb16 = sb("cb16", [128, 256], BF16)
    lnp = sb("lnp_t", [128, DEPTH * 4, KC], F32)
    convp = sb("convp_t", [128, 4, NJ2], F32)
    carry = sb("carry", [128, DEPTH, NJ2, 2], F32)
    sink_t = sb("sink_t", [128, N_B * NQ], F32)
    st = sb("st", [128, 64], F32)
    flg = sb("flg_t", [128, 2 * NTILES], F32)
    mask0 = sb("mask0", [128, 256], F32)
    SWB = 84 * 1024
    S = sb("S", [128, SWB // 4], F32)
    ps = [es.enter_context(nc.psum_tensor(f"ps{i}", [128, 512], F32)) for i in range(8)]

    ones32 = c32[:, 0:128]
    maskA = c32[:, 128:384]
    maskX = c32[:, 384:640]
    ident = cb16[:, 0:128]
    causal = cb16[:, 128:256]

    def view(off, shape, dt):
        n = int(np.prod(shape)); esz = 4 if dt == F32 else 2
        assert off % 4 == 0 and (n * esz) % 4 == 0 and off + n * esz <= SWB, (off, shape)
        a = S[:, off // 4: off // 4 + (n * esz) // 4]
        if dt != F32:
            a = a.bitcast(dt)
        if len(shape) == 2:
            a = a.rearrange("p (a b) -> p a b", a=shape[0])
        elif len(shape) == 3:
            a = a.rearrange("p (a b c) -> p a b c", a=shape[0], b=shape[1])
        return a

    state = {"bank": 0, "w": 0}
    bank_free = [None] * 8
    wfree = [None] * NW

    def next_bank():
        b = state["bank"]; state["bank"] = (b + 1) % 8
        return b

    def ACT(fn, deps=()):
        return P.emit("act", fn, deps, sig="act", serial=True)

    def DVE(fn, deps=()):
        return P.emit("dve", fn, deps, sig="dve", serial=True)

    def DMA(eng, key, fn, deps=()):
        return P.emit(eng, fn, deps, sig=key, inc=16)

    wflat = [w_[:].rearrange("p k c -> p (k c)") for w_ in wring]

    def load_w(tile_ap, pr, nk, ncols):
        slot = state["w"] % NW; state["w"] += 1
        key = f"w{slot}"
        dst = wflat[slot][0:pr, 0:nk * ncols]
        tok = DMA("pool", key, lambda e, d=dst, s=tile_ap: e.dma_start(out=d, in_=s), deps=[wfree[slot]])
        wv = dst.rearrange("p (k c) -> p k c", c=ncols)
        return slot, tok, wv

    def mm_group(out_ap, pairs, deps, bank, first=True, last=True, wslot=None):
        n = len(pairs)
        tok = None
        for i, (l, r) in enumerate(pairs):
            st_ = first and i == 0
            sp_ = last and i == n - 1
            dd = []
            if i == 0:
                dd = list(deps)
                if first:
                    dd.append(bank_free[bank])
            sig = "pe" if (i == n - 1) else None
            tok = P.emit("pe", lambda e, o=out_ap, l=l, r=r, a=st_, b=sp_: e.matmul(o, lhsT=l, rhs=r, start=a, stop=b),
                         deps=dd, sig=sig)
        if wslot is not None:
            wfree[wslot] = tok
        return tok

    def proj_fm(tile_ap, nk, ncols, rhs_fn, deps, pr=128):
        slot, wt, wv = load_w(tile_ap, pr, nk, ncols)
        b = next_bank()
        out = ps[b][0:ncols, 0:T]
        pairs = [(wv[:, k, :], rhs_fn(k)) for k in range(nk)]
        tok = mm_group(out, pairs, [wt] + list(deps), b, wslot=slot)
        return b, out, tok

    def DMA0(eng, key, fn):
        return P0.emit(eng, fn, (), sig=key, inc=16)
    DMA0("sp", "cst", lambda e: e.dma_start(out=c32[:], in_=c32_d))
    DMA0("sp", "cst", lambda e: e.dma_start(out=lnp[:].rearrange("p a k -> p (a k)"), in_=lnp_d))
    DMA0("sp", "cst", lambda e: e.dma_start(out=flg[:], in_=flg_d))
    t_sink = DMA0("sp", "cst", lambda e: e.dma_start(out=sink_t[:], in_=sink_d))
    t_cb = DMA0("pool", "cstp", lambda e: e.dma_start(out=cb16[:], in_=cb_d))
    P0.emit("dve", lambda e: e.memset(carry[:].rearrange("p a b c -> p (a b c)"), 0.0), (), sig="pdve")
    P0.emit("dve", lambda e: e.memset(kT[:].rearrange("p a b -> p (a b)"), 0.0), (), sig="pdve")
    t_ms = P0.emit("dve", lambda e: e.memset(Vt[:].rearrange("p a b -> p (a b)"), 0.0), (), sig="pdve")
    for e_ in ("pe", "act", "dve"):
        P.pending[e_] += [t_sink, t_cb, t_ms]

    hflat = h[:].rearrange("p k t -> p (k t)")
    hbflat = hb[:].rearrange("p k t -> p (k t)")

    def layer_norm(l, s, tok_y):
        P.barrier()
        ysq = view(0, [KC, T], F32)
        mean_sb = S[:, KC * T: KC * T + T]
        t1 = S[:, KC * T + T: KC * T + 2 * T]
        rstd = S[:, KC * T + 2 * T: KC * T + 3 * T]
        nmr = S[:, KC * T + 3 * T: KC * T + 4 * T]
        ta = ACT(lambda e: e.activation(out=ysq.rearrange("p k t -> p (k t)"), in_=hflat, func=AF.Square), deps=[tok_y])
        bm = next_bank(); bq = next_bank()
        mean_ps = ps[bm][:, 0:T]; msq_ps = ps[bq][:, 0:T]
        tm = mm_group(mean_ps, [(ones32, h[:, k, :]) for k in range(KC)], [tok_y], bm)
        tq = mm_group(msq_ps, [(ones32, ysq[:, k, :]) for k in range(KC)], [ta], bq)
        ta2 = ACT(lambda e: e.activation(out=mean_sb, in_=mean_ps, func=AF.Copy), deps=[tm])
        DVE(lambda e: e.tensor_tensor(out=t1, in0=mean_sb, in1=mean_sb, op=ALU.mult), deps=[ta2])
        DVE(lambda e: e.tensor_tensor(out=t1, in0=msq_ps, in1=t1, op=ALU.subtract), deps=[tq])
        tvar = DVE(lambda e: e.tensor_scalar(out=t1, in0=t1, scalar1=EPS, scalar2=None, op0=ALU.add))
        tsq = ACT(lambda e: e.activation(out=t1, in_=t1, func=AF.Sqrt), deps=[tvar])
        DVE(lambda e: e.reciprocal(out=rstd, in_=t1), deps=[tsq])
        td = DVE(lambda e: e.scalar_tensor_tensor(out=nmr, in0=mean_sb, scalar=-1.0, in1=rstd, op0=ALU.mult, op1=ALU.mult))
        bank_free[bm] = ta2; bank_free[bq] = td
        gi = (l * 2 + s) * 2
        rb = rstd.unsqueeze(1).to_broadcast([128, KC, T])
        nb_ = nmr.unsqueeze(1).to_broadcast([128, KC, T])
        gb_ = lnp[:, gi, :].unsqueeze(2).to_broadcast([128, KC, T])
        bb_ = lnp[:, gi + 1, :].unsqueeze(2).to_broadcast([128, KC, T])
        DVE(lambda e: e.tensor_tensor(out=h[:], in0=h[:], in1=rb, op=ALU.mult))
        DVE(lambda e: e.tensor_tensor(out=h[:], in0=h[:], in1=nb_, op=ALU.add))
        DVE(lambda e: e.tensor_tensor(out=h[:], in0=h[:], in1=gb_, op=ALU.mult))
        th = DVE(lambda e: e.tensor_tensor(out=h[:], in0=h[:], in1=bb_, op=ALU.add))
        thb = ACT(lambda e: e.activation(out=hbflat, in_=hflat, func=AF.Copy), deps=[th])
        return th, thb

    def resid_first(m, psum_ap, tok, bank):
        t = DVE(lambda e: e.scalar_tensor_tensor(out=h[:, m, :], in0=h[:, m, :], scalar=ALPHA, in1=psum_ap,
                                                 op0=ALU.mult, op1=ALU.add), deps=[tok])
        bank_free[bank] = t
        return t

    def resid_add(m, psum_ap, tok, bank):
        t = DVE(lambda e: e.tensor_tensor(out=h[:, m, :], in0=h[:, m, :], in1=psum_ap, op=ALU.add), deps=[tok])
        bank_free[bank] = t
        return t

    def gmlp(l, tok_hb):
        P.barrier()
        o = 0
        uT = view(o, [G, T], BF16); o += G * T * 2
        v32 = S[:, o // 4: o // 4 + DI]; o += DI * 4
        vbc = S[:, o // 4: o // 4 + DI // 2].bitcast(BF16); o += DI * 2
        WsT = view(o, [G, 128], BF16); o += G * 128 * 2
        HD = DI // 2
        gt = S[:, o // 4: o // 4 + HD]; o += HD * 4
        bt = S[:, o // 4: o // 4 + HD]; o += HD * 4
        bsb = S[:, o // 4: o // 4 + G * 128]; o += G * 128 * 4
        assert o <= o_tmp
        bar = [P.last["pe"], P.last["act"], P.last["dve"]]
        tws = DMA("pool", "sgu", lambda e: e.dma_start(out=WsT.rearrange("p g t -> p (g t)"),
                                                      in_=sguw_d[l * 128:(l + 1) * 128, :]), deps=bar)
        tbs = DMA("sp", "sgub", lambda e: e.dma_start(out=bsb, in_=sgub_d[l * 128:(l + 1) * 128, :]), deps=bar)
        cm = causal.unsqueeze(1).to_broadcast([128, G, 128])
        twm = DVE(lambda e: e.tensor_tensor(out=WsT, in0=WsT, in1=cm, op=ALU.mult), deps=[tws])
        for m in range(G):
            b, out, tok = proj_fm(w_in_a[(l * 2 * G + m) * 128:(l * 2 * G + m + 1) * 128, :], KC, 128, lambda k: hb[:, k, :], [tok_hb])
            ta = ACT(lambda e, m=m, out=out: e.activation(out=uT[:, m, :], in_=out, func=AF.Gelu), deps=[tok])
            bank_free[b] = ta
        tgb_use = None
        t_u_done = ta
        tvb = None
        for ch in range(TC):
            tv = None
            for nb4 in range(0, G, 4):
                b = next_bank()
                ng = min(4, G - nb4)
                tok = None
                for q_ in range(ng):
                    nbk = nb4 + q_
                    tix = l * 2 * G + G + nbk
                    slot, wt, wv = load_w(w_in_a[tix * 128:(tix + 1) * 128, :], 128, KC, 128)
                    pairs = [(hb[:, k, ch * 128:(ch + 1) * 128], wv[:, k, :]) for k in range(KC)]
                    dd = [wt, tok_hb]
                    if q_ > 0:
                        tok = mm_group(ps[b][:, q_ * 128:(q_ + 1) * 128], pairs, dd, b, first=True, wslot=slot)
                    else:
                        tok = mm_group(ps[b][:, 0:128], pairs, dd, b, first=True, wslot=slot)
                tv = ACT(lambda e, b=b, nb4=nb4, ng=ng: e.activation(out=v32[:, nb4 * 128:(nb4 + ng) * 128],
                                                                   in_=ps[b][:, 0:ng * 128], func=AF.Gelu),
                         deps=[tok, tgb_use])
                bank_free[b] = tv
            nchk = DI // 512 if DI >= 512 else 1
            cw = DI // nchk
            stats = st[:, 0:nchk * 6].rearrange("p (a b) -> p a b", a=nchk)
            for i in range(nchk):
                DVE(lambda e, i=i: e.bn_stats(out=stats[:, i, :], in_=v32[:, i * cw:(i + 1) * cw]), deps=[tv])
            mv = st[:, 48:50]
            DVE(lambda e: e.bn_aggr(out=mv, in_=stats))
            rs = st[:, 50:51]; nm = st[:, 51:52]
            tvar = DVE(lambda e: e.tensor_scalar(out=rs, in0=mv[:, 1:2], scalar1=EPS, scalar2=None, op0=ALU.add))
            tsq = ACT(lambda e: e.activation(out=rs, in_=rs, func=AF.Sqrt), deps=[tvar])
            DVE(lambda e: e.reciprocal(out=rs, in_=rs), deps=[tsq])
            tnm = DVE(lambda e: e.scalar_tensor_tensor(out=nm, in0=mv[:, 0:1], scalar=-1.0, in1=rs, op0=ALU.mult, op1=ALU.mult))
            tn = ACT(lambda e: e.activation(out=v32, in_=v32, func=AF.Identity, bias=nm, scale=rs), deps=[tnm])
            for hf in range(2):
                tg = DMA("sp", "vg", lambda e, hf=hf: e.dma_start(out=gt, in_=vgb_d[(l * 2) * 128:(l * 2 + 1) * 128, hf * HD:(hf + 1) * HD]),
                         deps=[tvb, bar[2]])
                tb = DMA("sp", "vg", lambda e, hf=hf: e.dma_start(out=bt, in_=vgb_d[(l * 2 + 1) * 128:(l * 2 + 2) * 128, hf * HD:(hf + 1) * HD]))
                DVE(lambda e, hf=hf: e.tensor_tensor(out=v32[:, hf * HD:(hf + 1) * HD], in0=v32[:, hf * HD:(hf + 1) * HD], in1=gt, op=ALU.mult),
                    deps=[tn, tb])
                tvb = DVE(lambda e, hf=hf: e.tensor_tensor(out=vbc[:, hf * HD:(hf + 1) * HD], in0=v32[:, hf * HD:(hf + 1) * HD], in1=bt, op=ALU.add))
            tgb_use = tvb
            tgate = None
            for g4 in range(0, G, 4):
                ng = min(4, G - g4)
                b = next_bank()
                tok = None
                for q_ in range(ng):
                    g_ = g4 + q_
                    tok = mm_group(ps[b][:, q_ * 128:(q_ + 1) * 128], [(vbc[:, g_ * 128:(g_ + 1) * 128], WsT[:, g_, :])],
                                   [tvb, twm], b)
                tmp = view(o_tmp, [4, 128], F32)
                pv = ps[b][:, 0:ng * 128].rearrange("p (g t) -> p g t", g=ng)
                bv = bsb[:, g4 * 128:(g4 + ng) * 128].rearrange("p (g t) -> p g t", g=ng)
                DVE(lambda e, pv=pv, bv=bv, ng=ng: e.tensor_tensor(out=tmp[:, 0:ng, :], in0=pv, in1=bv, op=ALU.add), deps=[tok, tbs, t_u_done])
                tgate = DVE(lambda e, g4=g4, ng=ng, ch=ch: e.tensor_tensor(out=uT[:, g4:g4 + ng, ch * 128:(ch + 1) * 128],
                                                                         in0=uT[:, g4:g4 + ng, ch * 128:(ch + 1) * 128],
                                                                         in1=tmp[:, 0:ng, :], op=ALU.mult))
                bank_free[b] = tgate
            tgb_use = tgate if tgate is not None else tgb_use
        tl = None
        for m in range(KC):
            b, out, tok = proj_fm(w_out_a[(l * KC + m) * 128:(l * KC + m + 1) * 128, :], G, 128, lambda k: uT[:, k, :], [tgate])
            tl = resid_first(m, out, tok, b)
        return tl

    o_tmp = SWB - 4 * 128 * 4

    def kv_proj(ti, tok_hb):
        P.barrier()
        DVE(lambda e: e.tensor_scalar(out=kT[0:64, :, 0:128], in0=kT[0:64, :, T:T + 128], scalar1=flg[0:64, bass.ds(2 * ctx["i"], 1)],
                                      scalar2=None, op0=ALU.mult))
        DVE(lambda e: e.tensor_scalar(out=Vt[:, 0, :], in0=Vt[:, TC, :], scalar1=flg[:, bass.ds(2 * ctx["i"], 1)],
                                      scalar2=None, op0=ALU.mult))
        tk = None
        for kvh in range(NKV):
            b, out, tok = proj_fm(w_k[kvh * 128:(kvh + 1) * 128, :], KC, 64, lambda k: hb[:, k, :], [tok_hb])
            tk = ACT(lambda e, kvh=kvh, out=out: e.activation(out=kT[0:64, kvh, 128:128 + T], in_=out, func=AF.Copy),
                     deps=[tok, P.last["dve"]])
            bank_free[b] = tk
        for ch in range(TC):
            for nbk in range(NVB):
                slot, wt, wv = load_w(w_v[nbk * 128:(nbk + 1) * 128, :], 128, KC, VB)
                b = next_bank()
                pairs = [(hb[:, k, ch * 128:(ch + 1) * 128], wv[:, k, :]) for k in range(KC)]
                tok = mm_group(ps[b][:, 0:VB], pairs, [wt, tok_hb], b, wslot=slot)
                tk = ACT(lambda e, b=b, ch=ch, nbk=nbk: e.activation(out=Vt[:, 1 + ch, nbk * VB:(nbk + 1) * VB], in_=ps[b][:, 0:VB], func=AF.Copy),
                         deps=[tok, P.last["dve"]])
                bank_free[b] = tk
        return tk

    def swa(j, ti, tok_hb, tok_kv):
        P.barrier()
        o = 0
        qT = view(o, [NQ, T], BF16); o += NQ * T * 2
        oT = view(o, [NQ, T], BF16); o += NQ * T * 2
        NB_ = 2
        s32 = [S[:, o // 4 + i * 256: o // 4 + (i + 1) * 256] for i in range(NB_)]; o += NB_ * 1024
        pn = [S[:, o // 4 + i * 128: o // 4 + (i + 1) * 128].bitcast(BF16) for i in range(NB_)]; o += NB_ * 512
        pT = [S[:, o // 4 + i * 128: o // 4 + (i + 1) * 128].bitcast(BF16) for i in range(NB_)]; o += NB_ * 512
        assert o <= SWB
        for hh in range(NQ):
            b, out, tok = proj_fm(w_in_b[(j * NQ + hh) * 128:(j * NQ + hh + 1) * 128, :], KC, 64, lambda k: hb[:, k, :], [tok_hb])
            tq = ACT(lambda e, hh=hh, out=out: e.activation(out=qT[0:64, hh, :], in_=out, func=AF.Copy, scale=0.125), deps=[tok])
            bank_free[b] = tq
        it = 0
        pn_free = [None] * NB_; pT_free = [None] * NB_
        to = None
        for hh in range(NQ):
            kvh = hh // 8
            sk = sink_t[:, j * NQ + hh: j * NQ + hh + 1]
            for blk in range(TC):
                i2 = it % NB_; it += 1
                so = 8 * i2
                mx = st[:, so:so + 1]; ngm = st[:, so + 1:so + 2]; rsum = st[:, so + 2:so + 3]
                esk = st[:, so + 3:so + 4]; den = st[:, so + 4:so + 5]; rinv = st[:, so + 5:so + 6]
                mask = mask0[:] if blk == 0 else maskA
                b = next_bank()
                sps = ps[b][:, 0:256]
                tok = mm_group(sps, [(qT[0:64, hh, blk * 128:(blk + 1) * 128], kT[0:64, kvh, blk * 128:blk * 128 + 256])],
                               [tq, tok_kv], b)
                t1_ = DVE(lambda e, i2=i2, sps=sps, mask=mask: e.tensor_tensor(out=s32[i2], in0=sps, in1=mask, op=ALU.add), deps=[tok])
                bank_free[b] = t1_
                DVE(lambda e, i2=i2, mx=mx: e.tensor_reduce(out=mx, in_=s32[i2], axis=AX.X, op=ALU.max))
                t2_ = DVE(lambda e, mx=mx, ngm=ngm, sk=sk: e.tensor_scalar(out=ngm, in0=mx, scalar1=sk, scalar2=-1.0, op0=ALU.max, op1=ALU.mult))
                ACT(lambda e, i2=i2, ngm=ngm, rsum=rsum: e.activation(out=s32[i2], in_=s32[i2], func=AF.Exp, bias=ngm, scale=1.0, accum_out=rsum), deps=[t2_])
                t3_ = ACT(lambda e, esk=esk, sk=sk, ngm=ngm: e.activation(out=esk, in_=sk, func=AF.Exp, bias=ngm, scale=1.0))
                DVE(lambda e, den=den, rsum=rsum, esk=esk: e.tensor_tensor(out=den, in0=rsum, in1=esk, op=ALU.add), deps=[t3_])
                DVE(lambda e, den=den, rinv=rinv: e.reciprocal(out=rinv, in_=den))
                t4_ = DVE(lambda e, i2=i2, rinv=rinv: e.tensor_scalar(out=pn[i2], in0=s32[i2], scalar1=rinv, scalar2=None, op0=ALU.mult),
                          deps=[pn_free[i2]])
                b2 = next_bank()
                ptp = ps[b2][:].bitcast(BF16)
                P.emit("pe", lambda e, i2=i2, ptp=ptp: e.transpose(ptp[:, 0:128], pn[i2][:, 0:128], ident), deps=[t4_, bank_free[b2]])
                t5_ = P.emit("pe", lambda e, i2=i2, ptp=ptp: e.transpose(ptp[:, 128:256], pn[i2][:, 128:256], ident), sig="pe")
                pn_free[i2] = t5_
                t6_ = ACT(lambda e, i2=i2, ptp=ptp: e.activation(out=pT[i2], in_=ptp[:, 0:256], func=AF.Copy), deps=[t5_, pT_free[i2]])
                bank_free[b2] = t6_
                b3 = next_bank()
                ops_ = ps[b3][0:64, 0:128]
                t7_ = mm_group(ops_, [(Vt[:, blk, kvh * 64:(kvh + 1) * 64], pT[i2][:, 0:128]),
                                      (Vt[:, blk + 1, kvh * 64:(kvh + 1) * 64], pT[i2][:, 128:256])], [t6_, tok_kv], b3)
                pT_free[i2] = t7_
                to = ACT(lambda e, hh=hh, blk=blk, ops_=ops_: e.activation(out=oT[0:64, hh, blk * 128:(blk + 1) * 128], in_=ops_, func=AF.Copy), deps=[t7_])
                bank_free[b3] = to
        tl = None
        for m in range(KC):
            b = next_bank()
            out = ps[b][:, 0:T]
            tok = None
            nparts = NPARTS
            for pi in range(nparts):
                tix = (j * KC + m) * NPARTS + pi
                slot, wt, wv = load_w(w_out_b[tix * 64:(tix + 1) * 64, :], 64, HB, 128)
                pairs = [(wv[:, k, :], oT[0:64, pi * HB + k, :]) for k in range(HB)]
                tok = mm_group(out, pairs, [wt, to], b, first=(pi == 0), last=(pi == nparts - 1), wslot=slot)
            tl = resid_first(m, out, tok, b)
        return tl

    def ffn(l, ti, tok_hb):
        P.barrier()
        GS = max(gsz)
        o = 0
        hid = view(o, [GS, T], BF16); o += GS * T * 2
        NZ = 2
        zb = [[S[:, o // 4 + (2 * i + w) * (T + 2): o // 4 + (2 * i + w + 1) * (T + 2)] for w in range(2)] for i in range(NZ)]
        o += NZ * 2 * (T + 2) * 4
        cg = [[S[:, o // 4 + (2 * i + w) * T: o // 4 + (2 * i + w + 1) * T] for w in range(2)] for i in range(NZ)]
        o += NZ * 2 * T * 4
        assert o <= SWB
        bar = [P.last["pe"], P.last["act"], P.last["dve"]]
        tcp = DMA("sp", "convp", lambda e: e.dma_start(out=convp[:].rearrange("p a j -> p (a j)"), in_=convp_d[l * 128:(l + 1) * 128, :]), deps=bar)
        it = 0
        zfree = [[None, None] for _ in range(NZ)]
        thid_free = None
        tl = None
        for gi, (j0, j1) in enumerate(groups):
            thid = None
            for j in range(j0, j1):
                i2 = it % NZ; it += 1
                tc_ = [None, None]
                for w in range(2):
                    jj = w * NJ + j
                    b, out, tok = proj_fm(w_up[(l * NJ2 + jj) * 128:(l * NJ2 + jj + 1) * 128, :], KC, 128, lambda k: hb[:, k, :], [tok_hb])
                    z = zb[i2][w]; cgt = cg[i2][w]
                    tz = ACT(lambda e, z=z, out=out: e.activation(out=z[:, 2:2 + T], in_=out, func=AF.Copy), deps=[tok, zfree[i2][w]])
                    bank_free[b] = tz
                    DVE(lambda e, z=z, jj=jj: e.tensor_copy(out=z[:, 0:2], in_=carry[:, l, jj, :]), deps=[zfree[i2][w]])
                    DVE(lambda e, z=z, jj=jj: e.tensor_copy(out=carry[:, l, jj, :], in_=z[:, T:T + 2]), deps=[tz])
                    DVE(lambda e, z=z, cgt=cgt, jj=jj: e.tensor_scalar(out=cgt, in0=z[:, 2:2 + T], scalar1=convp[:, 2, jj:jj + 1],
                                                                     scalar2=convp[:, 3, jj:jj + 1], op0=ALU.mult, op1=ALU.add), deps=[tcp])
                    DVE(lambda e, z=z, cgt=cgt, jj=jj: e.scalar_tensor_tensor(out=cgt, in0=z[:, 1:1 + T], scalar=convp[:, 1, jj:jj + 1], in1=cgt,
                                                                            op0=ALU.mult, op1=ALU.add))
                    tc_[w] = DVE(lambda e, z=z, cgt=cgt, jj=jj: e.scalar_tensor_tensor(out=cgt, in0=z[:, 0:T], scalar=convp[:, 0, jj:jj + 1], in1=cgt,
                                                                                     op0=ALU.mult, op1=ALU.add))
                    zfree[i2][w] = tc_[w]
                sg = cg[i2][0]; cu = cg[i2][1]
                tsg = ACT(lambda e, sg=sg: e.activation(out=sg, in_=sg, func=AF.Silu), deps=[tc_[0]])
                thid = DVE(lambda e, sg=sg, cu=cu, jl=j - j0: e.tensor_tensor(out=hid[:, jl, :], in0=sg, in1=cu, op=ALU.mult),
                           deps=[tsg, thid_free])
                zfree[i2][0] = thid; zfree[i2][1] = thid
            nk = j1 - j0
            for m in range(KC):
                b, out, tok = proj_fm(w_down[(l * KC + m) * 128:(l * KC + m + 1) * 128, j0 * 128:j1 * 128], nk, 128, lambda k: hid[:, k, :], [thid])
                tl = resid_first(m, out, tok, b) if gi == 0 else resid_add(m, out, tok, b)
                thid_free = tok
        return tl

    ti = None
    tx = DMA("sp", "x", lambda e: e.dma_start(out=hflat, in_=xT[bass.ds(ctx["i"] * 128, 128), :]))
    tok_hb = ACT(lambda e: e.activation(out=hbflat, in_=hflat, func=AF.Copy), deps=[tx])
    DVE(lambda e: e.tensor_scalar(out=carry[:].rearrange("p a b c -> p (a b c)"), in0=carry[:].rearrange("p a b c -> p (a b c)"),
                                  scalar1=flg[:, bass.ds(2 * ctx["i"], 1)], scalar2=None, op0=ALU.mult))
    DVE(lambda e: e.scalar_tensor_tensor(out=mask0[:], in0=maskX, scalar=flg[:, bass.ds(2 * ctx["i"] + 1, 1)], in1=maskA,
                                         op0=ALU.mult, op1=ALU.add))
    tok_kv = None
    import os
    _dbg = int(os.environ.get("KDBG", "99"))
    th = P.last["dve"]
    for l in range(min(DEPTH, _dbg)):
        if l < N_A:
            ty = gmlp(l, tok_hb)
        else:
            if l == N_A:
                tok_kv = kv_proj(ti, tok_hb)
            ty = swa(l - N_A, ti, tok_hb, tok_kv)
        th, tok_hb = layer_norm(l, 0, ty)
        ty = ffn(l, ti, tok_hb)
        th, tok_hb = layer_norm(l, 1, ty)
    t_store = DMA("sp", "y", lambda e: e.dma_start(out=yT[bass.ds(ctx["i"] * 128, 128), :], in_=hflat), deps=[th])
    fin = [P.last["pe"], P.last["act"], P.last["dve"], t_store]
    for e_ in ENGS:
        P.emit(e_, None, deps=fin)

    keys = set(P.cnt) | set(P0.cnt)
    sems = {k: es.enter_context(nc.semaphore(f"s_{k}")) for k in sorted(keys)}
    semB1 = es.enter_context(nc.semaphore("s_bar1"))
    semB2 = es.enter_context(nc.semaphore("s_bar2"))
    pool_keys = set()
    for wl, fn, sig, inc in P.ops["pool"]:
        if sig is not None:
            pool_keys.add(sig)
    clear_keys = sorted(set(P.cnt) - pool_keys)
    ENG_OBJ = {"pe": nc.tensor, "act": nc.scalar, "dve": nc.vector, "pool": nc.gpsimd, "sp": nc.sync}
    regs = {}
    for name in ENGS:
        for wl, fn, sig, inc in P.ops[name]:
            for k, v in wl:
                if k in pool_keys and (name, k) not in regs:
                    regs[(name, k)] = ENG_OBJ[name].alloc_register(f"r_{name}_{k}")

    def replay(prog, name, e, body):
        pos = {}
        for wl, fn, sig, inc in prog.ops[name]:
            for k, v in wl:
                if body and k in pool_keys:
                    r = regs[(name, k)]
                    e.reg_alu(r, r, v - pos.get(k, 0), ALU.add)
                    pos[k] = v
                    e.wait_ge(sems[k], r)
                else:
                    e.wait_ge(sems[k], v)
            if fn is None:
                continue
            ins = fn(e)
            if sig is not None:
                ins.then_inc(sems[sig], inc)
        if body:
            for (n_, k), r in regs.items():
                if n_ == name and P.cnt[k] - pos.get(k, 0) != 0:
                    e.reg_alu(r, r, P.cnt[k] - pos.get(k, 0), ALU.add)

    def loop_tail(name, e, i):
        e.sem_inc(semB1, 1)
        if name == "sp":
            e.wait_ge(semB1, (i + 1) * 5)
            for k in clear_keys:
                e.sem_clear(sems[k])
            e.sem_inc(semB2, 1)
        e.wait_ge(semB2, i + 1)

    def sections(prog, i):
        block = es2.enter_context(nc.Block())

        def mk(name):
            def _(e):
                if i is None:
                    for (n_, k), r in regs.items():
                        if n_ == name:
                            e.reg_mov(r, 0)
                replay(prog, name, e, i is not None)
                if i is not None:
                    loop_tail(name, e, i)
            return _

        block.tensor(mk("pe"))
        block.scalar(mk("act"))
        block.vector(mk("dve"))
        block.gpsimd(mk("pool"))
        block.sync(mk("sp"))

    with ExitStack() as es2:
        sections(P0, None)
    with nc.Fori(0, NTILES) as it_:
        ctx["i"] = it_
        with ExitStack() as es2:
            sections(P, it_)

    es.close()
    ninstr = {k: len(v) for k, v in P.ops.items()}
    return nc, ninstr


def host_prep(cfg, inputs, core):
    c = cfg
    D, DI, DFF, SEQ, NBC = c.D, c.DI, c.DFF, c.SEQ, c.NBC
    KC, G, NJ, NJ2, NQ = c.KC, c.G, c.NJ, c.NJ2, c.NQ
    f = lambda a: np.ascontiguousarray(np.asarray(a, dtype=np.float32))
    x = np.asarray(inputs["x"])[core * NBC:(core + 1) * NBC]
    m = {}
    NT, T = c.NT, c.T
    xt = x.reshape(NBC, NT, T, KC, 128)
    m["xT"] = f(np.transpose(xt, (0, 1, 4, 3, 2)).reshape(NBC * NT * 128, KC * T))
    fl = np.zeros((128, 2 * NBC * NT), np.float32)
    for i in range(NBC * NT):
        first = (i % NT == 0)
        fl[:, 2 * i] = 0.0 if first else 1.0
        fl[:, 2 * i + 1] = 1.0 if first else 0.0
    m["flg"] = fl
    return m


def shared_prep(cfg, inputs):
    c = cfg
    D, DI, DFF, SEQ = c.D, c.DI, c.DFF, c.SEQ
    KC, G, NJ, NJ2, NQ = c.KC, c.G, c.NJ, c.NJ2, c.NQ
    DEPTH, N_A, N_B = c.DEPTH, c.N_A, c.N_B
    f = lambda a: np.ascontiguousarray(np.asarray(a, dtype=np.float32))
    m = {}
    NKV, KVD = c.NKV, c.KVD
    VB = min(128, KVD); NVB = KVD // VB
    HB = 32 if NQ >= 32 else NQ; NPARTS = NQ // HB
    tr = lambda a, shape, perm, out: np.ascontiguousarray(np.transpose(np.asarray(a, dtype=np.float32).reshape(shape), perm)).reshape(out)
    m["w_in_a"] = tr(inputs["mix_in_a"], (N_A, KC, 128, 2 * G, 128), (0, 3, 2, 1, 4), (N_A * 2 * G * 128, KC * 128))
    m["w_out_a"] = tr(inputs["mix_out_a"], (N_A, G, 128, KC, 128), (0, 3, 2, 1, 4), (N_A * KC * 128, G * 128))
    wkv = np.asarray(inputs["w_kv"], dtype=np.float32)
    m["w_k"] = tr(wkv[:, :KVD], (KC, 128, NKV, 64), (2, 1, 0, 3), (NKV * 128, KC * 64))
    m["w_v"] = tr(wkv[:, KVD:], (KC, 128, NVB, VB), (2, 1, 0, 3), (NVB * 128, KC * VB))
    m["w_in_b"] = tr(inputs["mix_in_b"], (N_B, KC, 128, NQ, 64), (0, 3, 2, 1, 4), (N_B * NQ * 128, KC * 64))
    m["w_out_b"] = tr(inputs["mix_out_b"], (N_B, NPARTS, HB, 64, KC, 128), (0, 4, 1, 3, 2, 5), (N_B * KC * NPARTS * 64, HB * 128))
    m["w_up"] = tr(inputs["ffn_up"], (DEPTH, KC, 128, NJ2, 128), (0, 3, 2, 1, 4), (DEPTH * NJ2 * 128, KC * 128))
    m["w_down"] = tr(inputs["ffn_down"], (DEPTH, NJ, 128, KC, 128), (0, 3, 2, 1, 4), (DEPTH * KC * 128, NJ * 128))
    lg = f(inputs["ln_g"]).reshape(DEPTH, 2, KC, 128)
    lb = f(inputs["ln_b"]).reshape(DEPTH, 2, KC, 128)
    lnp = np.stack([lg, lb], axis=2)
    m["lnp"] = f(np.transpose(lnp, (4, 0, 1, 2, 3)).reshape(128, DEPTH * 4 * KC))
    cw = f(inputs["ffn_conv_w"]).reshape(DEPTH, 3, NJ2, 128)
    cbb = f(inputs["ffn_conv_b"]).reshape(DEPTH, 1, NJ2, 128)
    cp = np.concatenate([cw, cbb], axis=1)
    m["convp"] = f(np.transpose(cp, (0, 3, 1, 2)).reshape(DEPTH * 128, 4 * NJ2))
    m["sinkb"] = f(np.broadcast_to(f(inputs["sinks"]).reshape(1, N_B * NQ), (128, N_B * NQ)))
    vg = f(inputs["norm_v_a_g"]); vb = f(inputs["norm_v_a_b"])
    vgb = np.stack([vg, vb], axis=1)
    m["vgb"] = f(np.broadcast_to(vgb[:, :, None, :], (N_A, 2, 128, DI)).reshape(N_A * 2 * 128, DI))
    sb_ = f(inputs["sgu_b"]).reshape(N_A, 1, G * 128)
    m["sgub"] = f(np.broadcast_to(sb_, (N_A, 128, G * 128)).reshape(N_A * 128, G * 128))
    sw = f(inputs["sgu_w"])
    m["sguwT"] = f(np.transpose(sw, (0, 3, 1, 2)).reshape(N_A * 128, G * 128))
    qi = np.arange(128)[:, None]; kj = np.arange(256)[None, :]
    band = (kj > qi) & (kj <= qi + 128)
    NEG = -30000.0
    cst = np.zeros((128, 640), np.float32)
    cst[:, 0:128] = 1.0 / D
    cst[:, 128:384] = np.where(band, 0.0, NEG)
    cst[:, 384:640] = np.where(np.broadcast_to(kj < 128, (128, 256)), NEG, 0.0)
    m["cst32"] = cst
    cb = np.zeros((128, 256), np.float32)
    cb[:, 0:128] = np.eye(128, dtype=np.float32)
    s_ = np.arange(128)[:, None]; t_ = np.arange(128)[None, :]
    cb[:, 128:256] = (t_ >= s_).astype(np.float32)
    m["cstb"] = cb
    return m


_FULL = dict(D=4096, DFF=11008, SEQ=4096, NBC=2, NCORES=2)


def run(cfg, inputs, trace=False):
    nc, ninstr = build(cfg)
    shared = shared_prep(cfg, inputs)
    in_maps = []
    for core in range(cfg.NCORES):
        m = dict(shared)
        m.update(host_prep(cfg, inputs, core))
        in_maps.append(m)
    res = run_bass_kernel_spmd(nc, in_maps, core_ids=list(range(cfg.NCORES)), trace=trace)
    outs = []
    for core in range(cfg.NCORES):
        y = np.asarray(res.results[core]["yT"]).reshape(cfg.NBC, cfg.NT, 128, cfg.KC, cfg.T)
        outs.append(np.transpose(y, (0, 1, 4, 3, 2)).reshape(cfg.NBC, cfg.SEQ, cfg.D))
    return np.ascontiguousarray(np.concatenate(outs, axis=0).astype(np.float32)), res


def kernel(**inputs):
    cfg = Cfg(**_FULL)
    out, _ = run(cfg, inputs)
    return out
```

```python
import math
from contextlib import ExitStack
import numpy as np
import concourse.bass as bass
import concourse.mybir as mybir
from concourse.bass_utils import run_bass_kernel_spmd

F32 = mybir.dt.float32
BF16 = mybir.dt.bfloat16
AF = mybir.ActivationFunctionType
ALU = mybir.AluOpType
AX = mybir.AxisListType

ENGS = ("pe", "act", "dve", "pool", "sp")


class Cfg:
    def __init__(self, D, DFF, SEQ, NBC, NCORES):
        self.D = D; self.DI = D; self.DFF = DFF; self.SEQ = SEQ; self.NBC = NBC; self.NCORES = NCORES
        self.DEPTH = 4; self.N_A = 2; self.N_B = 2
        self.KC = D // 128; self.G = self.DI // 128; self.NJ = DFF // 128; self.NJ2 = 2 * self.NJ
        self.NQ = D // 64; self.NKV = self.NQ // 8; self.KVD = self.NKV * 64
        self.T = 256; self.TC = 2; self.NT = SEQ // self.T
        self.FG = 4 if self.NJ >= 8 else 2
        self.ALPHA = (2.0 * self.DEPTH) ** 0.25
        self.EPS = 1e-5


class Prog:
    def __init__(self):
        self.ops = {e: [] for e in ENGS}
        self.cnt = {}
        self.waited = {e: {} for e in ENGS}
        self.last = {e: None for e in ENGS}
        self.pending = {e: [] for e in ENGS}

    def emit(self, eng, fn, deps=(), sig=None, inc=1, serial=False):
        waits = {}
        alld = list(deps) + self.pending[eng]
        self.pending[eng] = []
        if serial and self.last[eng] is not None:
            alld.append(self.last[eng])
        for d in alld:
            if d is None:
                continue
            k, v = d
            if v > waits.get(k, 0):
                waits[k] = v
        wl = []
        wd = self.waited[eng]
        for k, v in waits.items():
            if v > wd.get(k, 0):
                wd[k] = v
                wl.append((k, v))
        tok = None
        if sig is not None:
            self.cnt[sig] = self.cnt.get(sig, 0) + inc
            tok = (sig, self.cnt[sig])
            self.last[eng] = tok
        self.ops[eng].append((wl, fn, sig, inc))
        return tok

    def barrier(self):
        toks = [self.last[e] for e in ("pe", "act", "dve")]
        for e in ("pe", "act", "dve"):
            self.pending[e] += toks


def build(cfg):
    c = cfg
    D, DI, DFF, SEQ, NBC = c.D, c.DI, c.DFF, c.SEQ, c.NBC
    KC, G, NJ, NJ2, NQ, NKV, KVD = c.KC, c.G, c.NJ, c.NJ2, c.NQ, c.NKV, c.KVD
    T, TC, NT, DEPTH, N_A, N_B = c.T, c.TC, c.NT, c.DEPTH, c.N_A, c.N_B
    ALPHA, EPS = c.ALPHA, c.EPS
    VB = min(128, KVD)
    NVB = KVD // VB
    gsz = [NJ // c.FG + (1 if i < NJ % c.FG else 0) for i in range(c.FG)]
    groups = []
    s0 = 0
    for g_ in gsz:
        groups.append((s0, s0 + g_)); s0 += g_
    KW = max(KC, G, max(gsz), 32 if NQ >= 32 else NQ)
    NW = 3

    nc = bass.Bass("TRN2", target_bir_lowering=False)

    def din(name, shape):
        return nc.dram_tensor(name, list(shape), F32, kind="ExternalInput").ap()

    NTILES = NBC * NT
    xT = din("xT", [NTILES * 128, KC * T])
    flg_d = din("flg", [128, 2 * NTILES])
    HB = 32 if NQ >= 32 else NQ
    NPARTS = NQ // HB
    w_in_a = din("w_in_a", [N_A * 2 * G * 128, KC * 128])
    w_out_a = din("w_out_a", [N_A * KC * 128, G * 128])
    w_k = din("w_k", [NKV * 128, KC * 64])
    w_v = din("w_v", [NVB * 128, KC * VB])
    w_in_b = din("w_in_b", [N_B * NQ * 128, KC * 64])
    w_out_b = din("w_out_b", [N_B * KC * NPARTS * 64, HB * 128])
    w_up = din("w_up", [DEPTH * NJ2 * 128, KC * 128])
    w_down = din("w_down", [DEPTH * KC * 128, NJ * 128])
    lnp_d = din("lnp", [128, DEPTH * 4 * KC])
    convp_d = din("convp", [DEPTH * 128, 4 * NJ2])
    sink_d = din("sinkb", [128, N_B * NQ])
    vgb_d = din("vgb", [N_A * 2 * 128, DI])
    sgub_d = din("sgub", [N_A * 128, G * 128])
    sguw_d = din("sguwT", [N_A * 128, G * 128])
    c32_d = din("cst32", [128, 640])
    cb_d = din("cstb", [128, 256])
    yT = nc.dram_tensor("yT", [NTILES * 128, KC * T], F32, kind="ExternalOutput").ap()

    P0 = Prog()
    P = Prog()
    ctx = {}
    es = ExitStack()
    sb = lambda n, s, d: es.enter_context(nc.sbuf_tensor(n, list(s), d))
    h = sb("h", [128, KC, T], F32)
    hb = sb("hb", [128, KC, T], BF16)
    kT = sb("kT", [128, NKV, 128 + T], BF16)
    Vt = sb("Vt", [128, 1 + TC, KVD], BF16)
    wring = [sb(f"wr{i}", [128, KW, 128], BF16) for i in range(NW)]
    c32 = sb("c32", [128, 640], F32)
    cb16 = sb("cb16", [128, 256], BF16)
    lnp = sb("lnp_t", [128, DEPTH * 4, KC], F32)
    convp = sb("convp_t", [128, 4, NJ2], F32)
    carry = sb("carry", [128, DEPTH, NJ2, 2], F32)
    sink_t = sb("sink_t", [128, N_B * NQ], F32)
    st = sb("st", [128, 64], F32)
    flg = sb("flg_t", [128, 2 * NTILES], F32)
    mask0 = sb("mask0", [128, 256], F32)
    SWB = 84 * 1024
    S = sb("S", [128, SWB // 4], F32)
    ps = [es.enter_context(nc.psum_tensor(f"ps{i}", [128, 512], F32)) for i in range(8)]

    ones32 = c32[:, 0:128]
    maskA = c32[:, 128:384]
    maskX = c32[:, 384:640]
    ident = cb16[:, 0:128]
    causal = cb16[:, 128:256]

    def view(off, shape, dt):
        n = int(np.prod(shape)); esz = 4 if dt == F32 else 2
        assert off % 4 == 0 and (n * esz) % 4 == 0 and off + n * esz <= SWB, (off, shape)
        a = S[:, off // 4: off // 4 + (n * esz) // 4]
        if dt != F32:
            a = a.bitcast(dt)
        if len(shape) == 2:
            a = a.rearrange("p (a b) -> p a b", a=shape[0])
        elif len(shape) == 3:
            a = a.rearrange("p (a b c) -> p a b c", a=shape[0], b=shape[1])
        return a

    state = {"bank": 0, "w": 0}
    bank_free = [None] * 8
    wfree = [None] * NW

    def next_bank():
        b = state["bank"]; state["bank"] = (b + 1) % 8
        return b

    def ACT(fn, deps=()):
        return P.emit("act", fn, deps, sig="act", serial=True)

    def DVE(fn, deps=()):
        return P.emit("dve", fn, deps, sig="dve", serial=True)

    def DMA(eng, key, fn, deps=()):
        return P.emit(eng, fn, deps, sig=key, inc=16)

    wflat = [w_[:].rearrange("p k c -> p (k c)") for w_ in wring]

    def load_w(tile_ap, pr, nk, ncols):
        slot = state["w"] % NW; state["w"] += 1
        key = f"w{slot}"
        dst = wflat[slot][0:pr, 0:nk * ncols]
        tok = DMA("pool", key, lambda e, d=dst, s=tile_ap: e.dma_start(out=d, in_=s), deps=[wfree[slot]])
        wv = dst.rearrange("p (k c) -> p k c", c=ncols)
        return slot, tok, wv

    def mm_group(out_ap, pairs, deps, bank, first=True, last=True, wslot=None):
        n = len(pairs)
        tok = None
        for i, (l, r) in enumerate(pairs):
            st_ = first and i == 0
            sp_ = last and i == n - 1
            dd = []
            if i == 0:
                dd = list(deps)
                if first:
                    dd.append(bank_free[bank])
            sig = "pe" if (i == n - 1) else None
            tok = P.emit("pe", lambda e, o=out_ap, l=l, r=r, a=st_, b=sp_: e.matmul(o, lhsT=l, rhs=r, start=a, stop=b),
                         deps=dd, sig=sig)
        if wslot is not None:
            wfree[wslot] = tok
        return tok

    def proj_fm(tile_ap, nk, ncols, rhs_fn, deps, pr=128):
        slot, wt, wv = load_w(tile_ap, pr, nk, ncols)
        b = next_bank()
        out = ps[b][0:ncols, 0:T]
        pairs = [(wv[:, k, :], rhs_fn(k)) for k in range(nk)]
        tok = mm_group(out, pairs, [wt] + list(deps), b, wslot=slot)
        return b, out, tok

    def DMA0(eng, key, fn):
        return P0.emit(eng, fn, (), sig=key, inc=16)
    DMA0("sp", "cst", lambda e: e.dma_start(out=c32[:], in_=c32_d))
    DMA0("sp", "cst", lambda e: e.dma_start(out=lnp[:].rearrange("p a k -> p (a k)"), in_=lnp_d))
    DMA0("sp", "cst", lambda e: e.dma_start(out=flg[:], in_=flg_d))
    t_sink = DMA0("sp", "cst", lambda e: e.dma_start(out=sink_t[:], in_=sink_d))
    t_cb = DMA0("pool", "cstp", lambda e: e.dma_start(out=cb16[:], in_=cb_d))
    P0.emit("dve", lambda e: e.memset(carry[:].rearrange("p a b c -> p (a b c)"), 0.0), (), sig="pdve")
    P0.emit("dve", lambda e: e.memset(kT[:].rearrange("p a b -> p (a b)"), 0.0), (), sig="pdve")
    t_ms = P0.emit("dve", lambda e: e.memset(Vt[:].rearrange("p a b -> p (a b)"), 0.0), (), sig="pdve")
    for e_ in ("pe", "act", "dve"):
        P.pending[e_] += [t_sink, t_cb, t_ms]

    hflat = h[:].rearrange("p k t -> p (k t)")
    hbflat = hb[:].rearrange("p k t -> p (k t)")

    def layer_norm(l, s, tok_y):
        P.barrier()
        ysq = view(0, [KC, T], F32)
        mean_sb = S[:, KC * T: KC * T + T]
        t1 = S[:, KC * T + T: KC * T + 2 * T]
        rstd = S[:, KC * T + 2 * T: KC * T + 3 * T]
        nmr = S[:, KC * T + 3 * T: KC * T + 4 * T]
        ta = ACT(lambda e: e.activation(out=ysq.rearrange("p k t -> p (k t)"), in_=hflat, func=AF.Square), deps=[tok_y])
        bm = next_bank(); bq = next_bank()
        mean_ps = ps[bm][:, 0:T]; msq_ps = ps[bq][:, 0:T]
        tm = mm_group(mean_ps, [(ones32, h[:, k, :]) for k in range(KC)], [tok_y], bm)
        tq = mm_group(msq_ps, [(ones32, ysq[:, k, :]) for k in range(KC)], [ta], bq)
        ta2 = ACT(lambda e: e.activation(out=mean_sb, in_=mean_ps, func=AF.Copy), deps=[tm])
        DVE(lambda e: e.tensor_tensor(out=t1, in0=mean_sb, in1=mean_sb, op=ALU.mult), deps=[ta2])
        DVE(lambda e: e.tensor_tensor(out=t1, in0=msq_ps, in1=t1, op=ALU.subtract), deps=[tq])
        tvar = DVE(lambda e: e.tensor_scalar(out=t1, in0=t1, scalar1=EPS, scalar2=None, op0=ALU.add))
        tsq = ACT(lambda e: e.activation(out=t1, in_=t1, func=AF.Sqrt), deps=[tvar])
        DVE(lambda e: e.reciprocal(out=rstd, in_=t1), deps=[tsq])
        td = DVE(lambda e: e.scalar_tensor_tensor(out=nmr, in0=mean_sb, scalar=-1.0, in1=rstd, op0=ALU.mult, op1=ALU.mult))
        bank_free[bm] = ta2; bank_free[bq] = td
        gi = (l * 2 + s) * 2
        rb = rstd.unsqueeze(1).to_broadcast([128, KC, T])
        nb_ = nmr.unsqueeze(1).to_broadcast([128, KC, T])
        gb_ = lnp[:, gi, :].unsqueeze(2).to_broadcast([128, KC, T])
        bb_ = lnp[:, gi + 1, :].unsqueeze(2).to_broadcast([128, KC, T])
        DVE(lambda e: e.tensor_tensor(out=h[:], in0=h[:], in1=rb, op=ALU.mult))
        DVE(lambda e: e.tensor_tensor(out=h[:], in0=h[:], in1=nb_, op=ALU.add))
        DVE(lambda e: e.tensor_tensor(out=h[:], in0=h[:], in1=gb_, op=ALU.mult))
        th = DVE(lambda e: e.tensor_tensor(out=h[:], in0=h[:], in1=bb_, op=ALU.add))
        thb = ACT(lambda e: e.activation(out=hbflat, in_=hflat, func=AF.Copy), deps=[th])
        return th, thb

    def resid_first(m, psum_ap, tok, bank):
        t = DVE(lambda e: e.scalar_tensor_tensor(out=h[:, m, :], in0=h[:, m, :], scalar=ALPHA, in1=psum_ap,
                                                 op0=ALU.mult, op1=ALU.add), deps=[tok])
        bank_free[bank] = t
        return t

    def resid_add(m, psum_ap, tok, bank):
        t = DVE(lambda e: e.tensor_tensor(out=h[:, m, :], in0=h[:, m, :], in1=psum_ap, op=ALU.add), deps=[tok])
        bank_free[bank] = t
        return t

    def gmlp(l, tok_hb):
        P.barrier()
        o = 0
        uT = view(o, [G, T], BF16); o += G * T * 2
        v32 = S[:, o // 4: o // 4 + DI]; o += DI * 4
        vbc = S[:, o // 4: o // 4 + DI // 2].bitcast(BF16); o += DI * 2
        WsT = view(o, [G, 128], BF16); o += G * 128 * 2
        HD = DI // 2
        gt = S[:, o // 4: o // 4 + HD]; o += HD * 4
        bt = S[:, o // 4: o // 4 + HD]; o += HD * 4
        bsb = S[:, o // 4: o // 4 + G * 128]; o += G * 128 * 4
        assert o <= o_tmp
        bar = [P.last["pe"], P.last["act"], P.last["dve"]]
        tws = DMA("pool", "sgu", lambda e: e.dma_start(out=WsT.rearrange("p g t -> p (g t)"),
                                                      in_=sguw_d[l * 128:(l + 1) * 128, :]), deps=bar)
        tbs = DMA("sp", "sgub", lambda e: e.dma_start(out=bsb, in_=sgub_d[l * 128:(l + 1) * 128, :]), deps=bar)
        cm = causal.unsqueeze(1).to_broadcast([128, G, 128])
        twm = DVE(lambda e: e.tensor_tensor(out=WsT, in0=WsT, in1=cm, op=ALU.mult), deps=[tws])
        for m in range(G):
            b, out, tok = proj_fm(w_in_a[(l * 2 * G + m) * 128:(l * 2 * G + m + 1) * 128, :], KC, 128, lambda k: hb[:, k, :], [tok_hb])
            ta = ACT(lambda e, m=m, out=out: e.activation(out=uT[:, m, :], in_=out, func=AF.Gelu), deps=[tok])
            bank_free[b] = ta
        tgb_use = None
        t_u_done = ta
        tvb = None
        for ch in range(TC):
            tv = None
            for nb4 in range(0, G, 4):
                b = next_bank()
                ng = min(4, G - nb4)
                tok = None
                for q_ in range(ng):
                    nbk = nb4 + q_
                    tix = l * 2 * G + G + nbk
                    slot, wt, wv = load_w(w_in_a[tix * 128:(tix + 1) * 128, :], 128, KC, 128)
                    pairs = [(hb[:, k, ch * 128:(ch + 1) * 128], wv[:, k, :]) for k in range(KC)]
                    dd = [wt, tok_hb]
                    if q_ > 0:
                        tok = mm_group(ps[b][:, q_ * 128:(q_ + 1) * 128], pairs, dd, b, first=True, wslot=slot)
                    else:
                        tok = mm_group(ps[b][:, 0:128], pairs, dd, b, first=True, wslot=slot)
                tv = ACT(lambda e, b=b, nb4=nb4, ng=ng: e.activation(out=v32[:, nb4 * 128:(nb4 + ng) * 128],
                                                                   in_=ps[b][:, 0:ng * 128], func=AF.Gelu),
                         deps=[tok, tgb_use])
                bank_free[b] = tv
            nchk = DI // 512 if DI >= 512 else 1
            cw = DI // nchk
            stats = st[:, 0:nchk * 6].rearrange("p (a b) -> p a b", a=nchk)
            for i in range(nchk):
                DVE(lambda e, i=i: e.bn_stats(out=stats[:, i, :], in_=v32[:, i * cw:(i + 1) * cw]), deps=[tv])
            mv = st[:, 48:50]
            DVE(lambda e: e.bn_aggr(out=mv, in_=stats))
            rs = st[:, 50:51]; nm = st[:, 51:52]
            tvar = DVE(lambda e: e.tensor_scalar(out=rs, in0=mv[:, 1:2], scalar1=EPS, scalar2=None, op0=ALU.add))
            tsq = ACT(lambda e: e.activation(out=rs, in_=rs, func=AF.Sqrt), deps=[tvar])
            DVE(lambda e: e.reciprocal(out=rs, in_=rs), deps=[tsq])
            tnm = DVE(lambda e: e.scalar_tensor_tensor(out=nm, in0=mv[:, 0:1], scalar=-1.0, in1=rs, op0=ALU.mult, op1=ALU.mult))
            tn = ACT(lambda e: e.activation(out=v32, in_=v32, func=AF.Identity, bias=nm, scale=rs), deps=[tnm])
            for hf in range(2):
                tg = DMA("sp", "vg", lambda e, hf=hf: e.dma_start(out=gt, in_=vgb_d[(l * 2) * 128:(l * 2 + 1) * 128, hf * HD:(hf + 1) * HD]),
                         deps=[tvb, bar[2]])
                tb = DMA("sp", "vg", lambda e, hf=hf: e.dma_start(out=bt, in_=vgb_d[(l * 2 + 1) * 128:(l * 2 + 2) * 128, hf * HD:(hf + 1) * HD]))
                DVE(lambda e, hf=hf: e.tensor_tensor(out=v32[:, hf * HD:(hf + 1) * HD], in0=v32[:, hf * HD:(hf + 1) * HD], in1=gt, op=ALU.mult),
                    deps=[tn, tb])
                tvb = DVE(lambda e, hf=hf: e.tensor_tensor(out=vbc[:, hf * HD:(hf + 1) * HD], in0=v32[:, hf * HD:(hf + 1) * HD], in1=bt, op=ALU.add))
            tgb_use = tvb
            tgate = None
            for g4 in range(0, G, 4):
                ng = min(4, G - g4)
                b = next_bank()
                tok = None
                for q_ in range(ng):
                    g_ = g4 + q_
                    tok = mm_group(ps[b][:, q_ * 128:(q_ + 1) * 128], [(vbc[:, g_ * 128:(g_ + 1) * 128], WsT[:, g_, :])],
                                   [tvb, twm], b)
                tmp = view(o_tmp, [4, 128], F32)
                pv = ps[b][:, 0:ng * 128].rearrange("p (g t) -> p g t", g=ng)
                bv = bsb[:, g4 * 128:(g4 + ng) * 128].rearrange("p (g t) -> p g t", g=ng)
                DVE(lambda e, pv=pv, bv=bv, ng=ng: e.tensor_tensor(out=tmp[:, 0:ng, :], in0=pv, in1=bv, op=ALU.add), deps=[tok, tbs, t_u_done])
                tgate = DVE(lambda e, g4=g4, ng=ng, ch=ch: e.tensor_tensor(out=uT[:, g4:g4 + ng, ch * 128:(ch + 1) * 128],
                                                                         in0=uT[:, g4:g4 + ng, ch * 128:(ch + 1) * 128],
                                                                         in1=tmp[:, 0:ng, :], op=ALU.mult))
                bank_free[b] = tgate
            tgb_use = tgate if tgate is not None else tgb_use
        tl = None
        for m in range(KC):
            b, out, tok = proj_fm(w_out_a[(l * KC + m) * 128:(l * KC + m + 1) * 128, :], G, 128, lambda k: uT[:, k, :], [tgate])
            tl = resid_first(m, out, tok, b)
        return tl

    o_tmp = SWB - 4 * 128 * 4

    def kv_proj(ti, tok_hb):
        P.barrier()
        DVE(lambda e: e.tensor_scalar(out=kT[0:64, :, 0:128], in0=kT[0:64, :, T:T + 128], scalar1=flg[0:64, bass.ds(2 * ctx["i"], 1)],
                                      scalar2=None, op0=ALU.mult))
        DVE(lambda e: e.tensor_scalar(out=Vt[:, 0, :], in0=Vt[:, TC, :], scalar1=flg[:, bass.ds(2 * ctx["i"], 1)],
                                      scalar2=None, op0=ALU.mult))
        tk = None
        for kvh in range(NKV):
            b, out, tok = proj_fm(w_k[kvh * 128:(kvh + 1) * 128, :], KC, 64, lambda k: hb[:, k, :], [tok_hb])
            tk = ACT(lambda e, kvh=kvh, out=out: e.activation(out=kT[0:64, kvh, 128:128 + T], in_=out, func=AF.Copy),
                     deps=[tok, P.last["dve"]])
            bank_free[b] = tk
        for ch in range(TC):
            for nbk in range(NVB):
                slot, wt, wv = load_w(w_v[nbk * 128:(nbk + 1) * 128, :], 128, KC, VB)
                b = next_bank()
                pairs = [(hb[:, k, ch * 128:(ch + 1) * 128], wv[:, k, :]) for k in range(KC)]
                tok = mm_group(ps[b][:, 0:VB], pairs, [wt, tok_hb], b, wslot=slot)
                tk = ACT(lambda e, b=b, ch=ch, nbk=nbk: e.activation(out=Vt[:, 1 + ch, nbk * VB:(nbk + 1) * VB], in_=ps[b][:, 0:VB], func=AF.Copy),
                         deps=[tok, P.last["dve"]])
                bank_free[b] = tk
        return tk

    def swa(j, ti, tok_hb, tok_kv):
        P.barrier()
        o = 0
        qT = view(o, [NQ, T], BF16); o += NQ * T * 2
        oT = view(o, [NQ, T], BF16); o += NQ * T * 2
        NB_ = 2
        s32 = [S[:, o // 4 + i * 256: o // 4 + (i + 1) * 256] for i in range(NB_)]; o += NB_ * 1024
        pn = [S[:, o // 4 + i * 128: o // 4 + (i + 1) * 128].bitcast(BF16) for i in range(NB_)]; o += NB_ * 512
        pT = [S[:, o // 4 + i * 128: o // 4 + (i + 1) * 128].bitcast(BF16) for i in range(NB_)]; o += NB_ * 512
        assert o <= SWB
        for hh in range(NQ):
            b, out, tok = proj_fm(w_in_b[(j * NQ + hh) * 128:(j * NQ + hh + 1) * 128, :], KC, 64, lambda k: hb[:, k, :], [tok_hb])
            tq = ACT(lambda e, hh=hh, out=out: e.activation(out=qT[0:64, hh, :], in_=out, func=AF.Copy, scale=0.125), deps=[tok])
            bank_free[b] = tq
        it = 0
        pn_free = [None] * NB_; pT_free = [None] * NB_
        to = None
        for hh in range(NQ):
            kvh = hh // 8
            sk = sink_t[:, j * NQ + hh: j * NQ + hh + 1]
            for blk in range(TC):
                i2 = it % NB_; it += 1
                so = 8 * i2
                mx = st[:, so:so + 1]; ngm = st[:, so + 1:so + 2]; rsum = st[:, so + 2:so + 3]
                esk = st[:, so + 3:so + 4]; den = st[:, so + 4:so + 5]; rinv = st[:, so + 5:so + 6]
                mask = mask0[:] if blk == 0 else maskA
                b = next_bank()
                sps = ps[b][:, 0:256]
                tok = mm_group(sps, [(qT[0:64, hh, blk * 128:(blk + 1) * 128], kT[0:64, kvh, blk * 128:blk * 128 + 256])],
                               [tq, tok_kv], b)
                t1_ = DVE(lambda e, i2=i2, sps=sps, mask=mask: e.tensor_tensor(out=s32[i2], in0=sps, in1=mask, op=ALU.add), deps=[tok])
                bank_free[b] = t1_
                DVE(lambda e, i2=i2, mx=mx: e.tensor_reduce(out=mx, in_=s32[i2], axis=AX.X, op=ALU.max))
                t2_ = DVE(lambda e, mx=mx, ngm=ngm, sk=sk: e.tensor_scalar(out=ngm, in0=mx, scalar1=sk, scalar2=-1.0, op0=ALU.max, op1=ALU.mult))
                ACT(lambda e, i2=i2, ngm=ngm, rsum=rsum: e.activation(out=s32[i2], in_=s32[i2], func=AF.Exp, bias=ngm, scale=1.0, accum_out=rsum), deps=[t2_])
                t3_ = ACT(lambda e, esk=esk, sk=sk, ngm=ngm: e.activation(out=esk, in_=sk, func=AF.Exp, bias=ngm, scale=1.0))
                DVE(lambda e, den=den, rsum=rsum, esk=esk: e.tensor_tensor(out=den, in0=rsum, in1=esk, op=ALU.add), deps=[t3_])
                DVE(lambda e, den=den, rinv=rinv: e.reciprocal(out=rinv, in_=den))
                t4_ = DVE(lambda e, i2=i2, rinv=rinv: e.tensor_scalar(out=pn[i2], in0=s32[i2], scalar1=rinv, scalar2=None, op0=ALU.mult),
                          deps=[pn_free[i2]])
                b2 = next_bank()
                ptp = ps[b2][:].bitcast(BF16)
                P.emit("pe", lambda e, i2=i2, ptp=ptp: e.transpose(ptp[:, 0:128], pn[i2][:, 0:128], ident), deps=[t4_, bank_free[b2]])
                t5_ = P.emit("pe", lambda e, i2=i2, ptp=ptp: e.transpose(ptp[:, 128:256], pn[i2][:, 128:256], ident), sig="pe")
                pn_free[i2] = t5_
                t6_ = ACT(lambda e, i2=i2, ptp=ptp: e.activation(out=pT[i2], in_=ptp[:, 0:256], func=AF.Copy), deps=[t5_, pT_free[i2]])
                bank_free[b2] = t6_
                b3 = next_bank()
                ops_ = ps[b3][0:64, 0:128]
                t7_ = mm_group(ops_, [(Vt[:, blk, kvh * 64:(kvh + 1) * 64], pT[i2][:, 0:128]),
                                      (Vt[:, blk + 1, kvh * 64:(kvh + 1) * 64], pT[i2][:, 128:256])], [t6_, tok_kv], b3)
                pT_free[i2] = t7_
                to = ACT(lambda e, hh=hh, blk=blk, ops_=ops_: e.activation(out=oT[0:64, hh, blk * 128:(blk + 1) * 128], in_=ops_, func=AF.Copy), deps=[t7_])
                bank_free[b3] = to
        tl = None
        for m in range(KC):
            b = next_bank()
            out = ps[b][:, 0:T]
            tok = None
            nparts = NPARTS
            for pi in range(nparts):
                tix = (j * KC + m) * NPARTS + pi
                slot, wt, wv = load_w(w_out_b[tix * 64:(tix + 1) * 64, :], 64, HB, 128)
                pairs = [(wv[:, k, :], oT[0:64, pi * HB + k, :]) for k in range(HB)]
                tok = mm_group(out, pairs, [wt, to], b, first=(pi == 0), last=(pi == nparts - 1), wslot=slot)
            tl = resid_first(m, out, tok, b)
        return tl

    def ffn(l, ti, tok_hb):
        P.barrier()
        GS = max(gsz)
        o = 0
        hid = view(o, [GS, T], BF16); o += GS * T * 2
        NZ = 2
        zb = [[S[:, o // 4 + (2 * i + w) * (T + 2): o // 4 + (2 * i + w + 1) * (T + 2)] for w in range(2)] for i in range(NZ)]
        o += NZ * 2 * (T + 2) * 4
        cg = [[S[:, o // 4 + (2 * i + w) * T: o // 4 + (2 * i + w + 1) * T] for w in range(2)] for i in range(NZ)]
        o += NZ * 2 * T * 4
        assert o <= SWB
        bar = [P.last["pe"], P.last["act"], P.last["dve"]]
        tcp = DMA("sp", "convp", lambda e: e.dma_start(out=convp[:].rearrange("p a j -> p (a j)"), in_=convp_d[l * 128:(l + 1) * 128, :]), deps=bar)
        it = 0
        zfree = [[None, None] for _ in range(NZ)]
        thid_free = None
        tl = None
        for gi, (j0, j1) in enumerate(groups):
            thid = None
            for j in range(j0, j1):
                i2 = it % NZ; it += 1
                tc_ = [None, None]
                for w in range(2):
                    jj = w * NJ + j
                    b, out, tok = proj_fm(w_up[(l * NJ2 + jj) * 128:(l * NJ2 + jj + 1) * 128, :], KC, 128, lambda k: hb[:, k, :], [tok_hb])
                    z = zb[i2][w]; cgt = cg[i2][w]
                    tz = ACT(lambda e, z=z, out=out: e.activation(out=z[:, 2:2 + T], in_=out, func=AF.Copy), deps=[tok, zfree[i2][w]])
                    bank_free[b] = tz
                    DVE(lambda e, z=z, jj=jj: e.tensor_copy(out=z[:, 0:2], in_=carry[:, l, jj, :]), deps=[zfree[i2][w]])
                    DVE(lambda e, z=z, jj=jj: e.tensor_copy(out=carry[:, l, jj, :], in_=z[:, T:T + 2]), deps=[tz])
                    DVE(lambda e, z=z, cgt=cgt, jj=jj: e.tensor_scalar(out=cgt, in0=z[:, 2:2 + T], scalar1=convp[:, 2, jj:jj + 1],
                                                                     scalar2=convp[:, 3, jj:jj + 1], op0=ALU.mult, op1=ALU.add), deps=[tcp])
                    DVE(lambda e, z=z, cgt=cgt, jj=jj: e.scalar_tensor_tensor(out=cgt, in0=z[:, 1:1 + T], scalar=convp[:, 1, jj:jj + 1], in1=cgt,
                                                                            op0=ALU.mult, op1=ALU.add))
                    tc_[w] = DVE(lambda e, z=z, cgt=cgt, jj=jj: e.scalar_tensor_tensor(out=cgt, in0=z[:, 0:T], scalar=convp[:, 0, jj:jj + 1], in1=cgt,
                                                                                     op0=ALU.mult, op1=ALU.add))
                    zfree[i2][w] = tc_[w]
                sg = cg[i2][0]; cu = cg[i2][1]
                tsg = ACT(lambda e, sg=sg: e.activation(out=sg, in_=sg, func=AF.Silu), deps=[tc_[0]])
                thid = DVE(lambda e, sg=sg, cu=cu, jl=j - j0: e.tensor_tensor(out=hid[:, jl, :], in0=sg, in1=cu, op=ALU.mult),
                           deps=[tsg, thid_free])
                zfree[i2][0] = thid; zfree[i2][1] = thid
            nk = j1 - j0
            for m in range(KC):
                b, out, tok = proj_fm(w_down[(l * KC + m) * 128:(l * KC + m + 1) * 128, j0 * 128:j1 * 128], nk, 128, lambda k: hid[:, k, :], [thid])
                tl = resid_first(m, out, tok, b) if gi == 0 else resid_add(m, out, tok, b)
                thid_free = tok
        return tl

    ti = None
    tx = DMA("sp", "x", lambda e: e.dma_start(out=hflat, in_=xT[bass.ds(ctx["i"] * 128, 128), :]))
    tok_hb = ACT(lambda e: e.activation(out=hbflat, in_=hflat, func=AF.Copy), deps=[tx])
    DVE(lambda e: e.tensor_scalar(out=carry[:].rearrange("p a b c -> p (a b c)"), in0=carry[:].rearrange("p a b c -> p (a b c)"),
                                  scalar1=flg[:, bass.ds(2 * ctx["i"], 1)], scalar2=None, op0=ALU.mult))
    DVE(lambda e: e.scalar_tensor_tensor(out=mask0[:], in0=maskX, scalar=flg[:, bass.ds(2 * ctx["i"] + 1, 1)], in1=maskA,
                                         op0=ALU.mult, op1=ALU.add))
    tok_kv = None
    import os
    _dbg = int(os.environ.get("KDBG", "99"))
    th = P.last["dve"]
    for l in range(min(DEPTH, _dbg)):
        if l < N_A:
            ty = gmlp(l, tok_hb)
        else:
            if l == N_A:
                tok_kv = kv_proj(ti, tok_hb)
            ty = swa(l - N_A, ti, tok_hb, tok_kv)
        th, tok_hb = layer_norm(l, 0, ty)
        ty = ffn(l, ti, tok_hb)
        th, tok_hb = layer_norm(l, 1, ty)
    t_store = DMA("sp", "y", lambda e: e.dma_start(out=yT[bass.ds(ctx["i"] * 128, 128), :], in_=hflat), deps=[th])
    fin = [P.last["pe"], P.last["act"], P.last["dve"], t_store]
    for e_ in ENGS:
        P.emit(e_, None, deps=fin)

    keys = set(P.cnt) | set(P0.cnt)
    sems = {k: es.enter_context(nc.semaphore(f"s_{k}")) for k in sorted(keys)}
    semB1 = es.enter_context(nc.semaphore("s_bar1"))
    semB2 = es.enter_context(nc.semaphore("s_bar2"))
    pool_keys = set()
    for wl, fn, sig, inc in P.ops["pool"]:
        if sig is not None:
            pool_keys.add(sig)
    clear_keys = sorted(set(P.cnt) - pool_keys)
    ENG_OBJ = {"pe": nc.tensor, "act": nc.scalar, "dve": nc.vector, "pool": nc.gpsimd, "sp": nc.sync}
    regs = {}
    for name in ENGS:
        for wl, fn, sig, inc in P.ops[name]:
            for k, v in wl:
                if k in pool_keys and (name, k) not in regs:
                    regs[(name, k)] = ENG_OBJ[name].alloc_register(f"r_{name}_{k}")

    def replay(prog, name, e, body):
        pos = {}
        for wl, fn, sig, inc in prog.ops[name]:
            for k, v in wl:
                if body and k in pool_keys:
                    r = regs[(name, k)]
                    e.reg_alu(r, r, v - pos.get(k, 0), ALU.add)
                    pos[k] = v
                    e.wait_ge(sems[k], r)
                else:
                    e.wait_ge(sems[k], v)
            if fn is None:
                continue
            ins = fn(e)
            if sig is not None:
                ins.then_inc(sems[sig], inc)
        if body:
            for (n_, k), r in regs.items():
                if n_ == name and P.cnt[k] - pos.get(k, 0) != 0:
                    e.reg_alu(r, r, P.cnt[k] - pos.get(k, 0), ALU.add)

    def loop_tail(name, e, i):
        e.sem_inc(semB1, 1)
        if name == "sp":
            e.wait_ge(semB1, (i + 1) * 5)
            for k in clear_keys:
                e.sem_clear(sems[k])
            e.sem_inc(semB2, 1)
        e.wait_ge(semB2, i + 1)

    def sections(prog, i):
        block = es2.enter_context(nc.Block())

        def mk(name):
            def _(e):
                if i is None:
                    for (n_, k), r in regs.items():
                        if n_ == name:
                            e.reg_mov(r, 0)
                replay(prog, name, e, i is not None)
                if i is not None:
                    loop_tail(name, e, i)
            return _

        block.tensor(mk("pe"))
        block.scalar(mk("act"))
        block.vector(mk("dve"))
        block.gpsimd(mk("pool"))
        block.sync(mk("sp"))

    with ExitStack() as es2:
        sections(P0, None)
    with nc.Fori(0, NTILES) as it_:
        ctx["i"] = it_
        with ExitStack() as es2:
            sections(P, it_)

    es.close()
    ninstr = {k: len(v) for k, v in P.ops.items()}
    return nc, ninstr


def host_prep(cfg, inputs, core):
    c = cfg
    D, DI, DFF, SEQ, NBC = c.D, c.DI, c.DFF, c.SEQ, c.NBC
    KC, G, NJ, NJ2, NQ = c.KC, c.G, c.NJ, c.NJ2, c.NQ
    f = lambda a: np.ascontiguousarray(np.asarray(a, dtype=np.float32))
    x = np.asarray(inputs["x"])[core * NBC:(core + 1) * NBC]
    m = {}
    NT, T = c.NT, c.T
    xt = x.reshape(NBC, NT, T, KC, 128)
    m["xT"] = f(np.transpose(xt, (0, 1, 4, 3, 2)).reshape(NBC * NT * 128, KC * T))
    fl = np.zeros((128, 2 * NBC * NT), np.float32)
    for i in range(NBC * NT):
        first = (i % NT == 0)
        fl[:, 2 * i] = 0.0 if first else 1.0
        fl[:, 2 * i + 1] = 1.0 if first else 0.0
    m["flg"] = fl
    return m


def shared_prep(cfg, inputs):
    c = cfg
    D, DI, DFF, SEQ = c.D, c.DI, c.DFF, c.SEQ
    KC, G, NJ, NJ2, NQ = c.KC, c.G, c.NJ, c.NJ2, c.NQ
    DEPTH, N_A, N_B = c.DEPTH, c.N_A, c.N_B
    f = lambda a: np.ascontiguousarray(np.asarray(a, dtype=np.float32))
    m = {}
    NKV, KVD = c.NKV, c.KVD
    VB = min(128, KVD); NVB = KVD // VB
    HB = 32 if NQ >= 32 else NQ; NPARTS = NQ // HB
    tr = lambda a, shape, perm, out: np.ascontiguousarray(np.transpose(np.asarray(a, dtype=np.float32).reshape(shape), perm)).reshape(out)
    m["w_in_a"] = tr(inputs["mix_in_a"], (N_A, KC, 128, 2 * G, 128), (0, 3, 2, 1, 4), (N_A * 2 * G * 128, KC * 128))
    m["w_out_a"] = tr(inputs["mix_out_a"], (N_A, G, 128, KC, 128), (0, 3, 2, 1, 4), (N_A * KC * 128, G * 128))
    wkv = np.asarray(inputs["w_kv"], dtype=np.float32)
    m["w_k"] = tr(wkv[:, :KVD], (KC, 128, NKV, 64), (2, 1, 0, 3), (NKV * 128, KC * 64))
    m["w_v"] = tr(wkv[:, KVD:], (KC, 128, NVB, VB), (2, 1, 0, 3), (NVB * 128, KC * VB))
    m["w_in_b"] = tr(inputs["mix_in_b"], (N_B, KC, 128, NQ, 64), (0, 3, 2, 1, 4), (N_B * NQ * 128, KC * 64))
    m["w_out_b"] = tr(inputs["mix_out_b"], (N_B, NPARTS, HB, 64, KC, 128), (0, 4, 1, 3, 2, 5), (N_B * KC * NPARTS * 64, HB * 128))
    m["w_up"] = tr(inputs["ffn_up"], (DEPTH, KC, 128, NJ2, 128), (0, 3, 2, 1, 4), (DEPTH * NJ2 * 128, KC * 128))
    m["w_down"] = tr(inputs["ffn_down"], (DEPTH, NJ, 128, KC, 128), (0, 3, 2, 1, 4), (DEPTH * KC * 128, NJ * 128))
    lg = f(inputs["ln_g"]).reshape(DEPTH, 2, KC, 128)
    lb = f(inputs["ln_b"]).reshape(DEPTH, 2, KC, 128)
    lnp = np.stack([lg, lb], axis=2)
    m["lnp"] = f(np.transpose(lnp, (4, 0, 1, 2, 3)).reshape(128, DEPTH * 4 * KC))
    cw = f(inputs["ffn_conv_w"]).reshape(DEPTH, 3, NJ2, 128)
    cbb = f(inputs["ffn_conv_b"]).reshape(DEPTH, 1, NJ2, 128)
    cp = np.concatenate([cw, cbb], axis=1)
    m["convp"] = f(np.transpose(cp, (0, 3, 1, 2)).reshape(DEPTH * 128, 4 * NJ2))
    m["sinkb"] = f(np.broadcast_to(f(inputs["sinks"]).reshape(1, N_B * NQ), (128, N_B * NQ)))
    vg = f(inputs["norm_v_a_g"]); vb = f(inputs["norm_v_a_b"])
    vgb = np.stack([vg, vb], axis=1)
    m["vgb"] = f(np.broadcast_to(vgb[:, :, None, :], (N_A, 2, 128, DI)).reshape(N_A * 2 * 128, DI))
    sb_ = f(inputs["sgu_b"]).reshape(N_A, 1, G * 128)
    m["sgub"] = f(np.broadcast_to(sb_, (N_A, 128, G * 128)).reshape(N_A * 128, G * 128))
    sw = f(inputs["sgu_w"])
    m["sguwT"] = f(np.transpose(sw, (0, 3, 1, 2)).reshape(N_A * 128, G * 128))
    qi = np.arange(128)[:, None]; kj = np.arange(256)[None, :]
    band = (kj > qi) & (kj <= qi + 128)
    NEG = -30000.0
    cst = np.zeros((128, 640), np.float32)
    cst[:, 0:128] = 1.0 / D
    cst[:, 128:384] = np.where(band, 0.0, NEG)
    cst[:, 384:640] = np.where(np.broadcast_to(kj < 128, (128, 256)), NEG, 0.0)
    m["cst32"] = cst
    cb = np.zeros((128, 256), np.float32)
    cb[:, 0:128] = np.eye(128, dtype=np.float32)
    s_ = np.arange(128)[:, None]; t_ = np.arange(128)[None, :]
    cb[:, 128:256] = (t_ >= s_).astype(np.float32)
    m["cstb"] = cb
    return m


_FULL = dict(D=4096, DFF=11008, SEQ=4096, NBC=1, NCORES=4)


def run(cfg, inputs, trace=False):
    nc, ninstr = build(cfg)
    shared = shared_prep(cfg, inputs)
    in_maps = []
    for core in range(cfg.NCORES):
        m = dict(shared)
        m.update(host_prep(cfg, inputs, core))
        in_maps.append(m)
    res = run_bass_kernel_spmd(nc, in_maps, core_ids=list(range(cfg.NCORES)), trace=trace)
    outs = []
    for core in range(cfg.NCORES):
        y = np.asarray(res.results[core]["yT"]).reshape(cfg.NBC, cfg.NT, 128, cfg.KC, cfg.T)
        outs.append(np.transpose(y, (0, 1, 4, 3, 2)).reshape(cfg.NBC, cfg.SEQ, cfg.D))
    return np.ascontiguousarray(np.concatenate(outs, axis=0).astype(np.float32)), res


def kernel(**inputs):
    cfg = Cfg(**_FULL)
    out, _ = run(cfg, inputs)
    return out
```

```python
import math
from contextlib import ExitStack
import numpy as np
import concourse.bass as bass
import concourse.mybir as mybir
from concourse.bass_utils import run_bass_kernel_spmd

F32 = mybir.dt.float32
BF16 = mybir.dt.bfloat16
AF = mybir.ActivationFunctionType
ALU = mybir.AluOpType
AX = mybir.AxisListType

ENGS = ("pe", "act", "dve", "pool", "sp")


class Cfg:
    def __init__(self, D, DFF, SEQ, NBC, NCORES):
        self.D = D; self.DI = D; self.DFF = DFF; self.SEQ = SEQ; self.NBC = NBC; self.NCORES = NCORES
        self.DEPTH = 4; self.N_A = 2; self.N_B = 2
        self.KC = D // 128; self.G = self.DI // 128; self.NJ = DFF // 128; self.NJ2 = 2 * self.NJ
        self.NQ = D // 64; self.NKV = self.NQ // 8; self.KVD = self.NKV * 64
        self.T = 256; self.TC = 2; self.NT = SEQ // self.T
        self.FG = 4 if self.NJ >= 8 else 2
        self.ALPHA = (2.0 * self.DEPTH) ** 0.25
        self.EPS = 1e-5


class Prog:
    def __init__(self):
        self.ops = {e: [] for e in ENGS}
        self.cnt = {}
        self.waited = {e: {} for e in ENGS}
        self.last = {e: None for e in ENGS}
        self.pending = {e: [] for e in ENGS}

    def emit(self, eng, fn, deps=(), sig=None, inc=1, serial=False):
        waits = {}
        alld = list(deps) + self.pending[eng]
        self.pending[eng] = []
        if serial and self.last[eng] is not None:
            alld.append(self.last[eng])
        for d in alld:
            if d is None:
                continue
            k, v = d
            if v > waits.get(k, 0):
                waits[k] = v
        wl = []
        wd = self.waited[eng]
        for k, v in waits.items():
            if v > wd.get(k, 0):
                wd[k] = v
                wl.append((k, v))
        tok = None
        if sig is not None:
            self.cnt[sig] = self.cnt.get(sig, 0) + inc
            tok = (sig, self.cnt[sig])
            self.last[eng] = tok
        self.ops[eng].append((wl, fn, sig, inc))
        return tok

    def barrier(self):
        toks = [self.last[e] for e in ("pe", "act", "dve")]
        for e in ("pe", "act", "dve"):
            self.pending[e] += toks


def build(cfg):
    c = cfg
    D, DI, DFF, SEQ, NBC = c.D, c.DI, c.DFF, c.SEQ, c.NBC
    KC, G, NJ, NJ2, NQ, NKV, KVD = c.KC, c.G, c.NJ, c.NJ2, c.NQ, c.NKV, c.KVD
    T, TC, NT, DEPTH, N_A, N_B = c.T, c.TC, c.NT, c.DEPTH, c.N_A, c.N_B
    ALPHA, EPS = c.ALPHA, c.EPS
    VB = min(128, KVD)
    NVB = KVD // VB
    gsz = [NJ // c.FG + (1 if i < NJ % c.FG else 0) for i in range(c.FG)]
    groups = []
    s0 = 0
    for g_ in gsz:
        groups.append((s0, s0 + g_)); s0 += g_
    KW = max(KC, G, max(gsz), 32 if NQ >= 32 else NQ)
    NW = 3

    nc = bass.Bass("TRN2", target_bir_lowering=False)

    def din(name, shape):
        return nc.dram_tensor(name, list(shape), F32, kind="ExternalInput").ap()

    NTILES = NBC * NT
    xT = din("xT", [NTILES * 128, KC * T])
    flg_d = din("flg", [128, 2 * NTILES])
    HB = 32 if NQ >= 32 else NQ
    NPARTS = NQ // HB
    w_in_a = din("w_in_a", [N_A * 2 * G * 128, KC * 128])
    w_out_a = din("w_out_a", [N_A * KC * 128, G * 128])
    w_k = din("w_k", [NKV * 128, KC * 64])
    w_v = din("w_v", [NVB * 128, KC * VB])
    w_in_b = din("w_in_b", [N_B * NQ * 128, KC * 64])
    w_out_b = din("w_out_b", [N_B * KC * NPARTS * 64, HB * 128])
    w_up = din("w_up", [DEPTH * NJ2 * 128, KC * 128])
    w_down = din("w_down", [DEPTH * KC * 128, NJ * 128])
    lnp_d = din("lnp", [128, DEPTH * 4 * KC])
    convp_d = din("convp", [DEPTH * 128, 4 * NJ2])
    sink_d = din("sinkb", [128, N_B * NQ])
    vgb_d = din("vgb", [N_A * 2 * 128, DI])
    sgub_d = din("sgub", [N_A * 128, G * 128])
    sguw_d = din("sguwT", [N_A * 128, G * 128])
    c32_d = din("cst32", [128, 640])
    cb_d = din("cstb", [128, 256])
    yT = nc.dram_tensor("yT", [NTILES * 128, KC * T], F32, kind="ExternalOutput").ap()

    P0 = Prog()
    P = Prog()
    ctx = {}
    es = ExitStack()
    sb = lambda n, s, d: es.enter_context(nc.sbuf_tensor(n, list(s), d))
    h = sb("h", [128, KC, T], F32)
    hb = sb("hb", [128, KC, T], BF16)
    kT = sb("kT", [128, NKV, 128 + T], BF16)
    Vt = sb("Vt", [128, 1 + TC, KVD], BF16)
    wring = [sb(f"wr{i}", [128, KW, 128], BF16) for i in range(NW)]
    c32 = sb("c32", [128, 640], F32)
    cb16 = sb("cb16", [128, 256], BF16)
    lnp = sb("lnp_t", [128, DEPTH * 4, KC], F32)
    convp = sb("convp_t", [128, 4, NJ2], F32)
    carry = sb("carry", [128, DEPTH, NJ2, 2], F32)
    sink_t = sb("sink_t", [128, N_B * NQ], F32)
    st = sb("st", [128, 64], F32)
    flg = sb("flg_t", [128, 2 * NTILES], F32)
    mask0 = sb("mask0", [128, 256], F32)
    SWB = 84 * 1024
    S = sb("S", [128, SWB // 4], F32)
    ps = [es.enter_context(nc.psum_tensor(f"ps{i}", [128, 512], F32)) for i in range(8)]

    ones32 = c32[:, 0:128]
    maskA = c32[:, 128:384]
    maskX = c32[:, 384:640]
    ident = cb16[:, 0:128]
    causal = cb16[:, 128:256]

    def view(off, shape, dt):
        n = int(np.prod(shape)); esz = 4 if dt == F32 else 2
        assert off % 4 == 0 and (n * esz) % 4 == 0 and off + n * esz <= SWB, (off, shape)
        a = S[:, off // 4: off // 4 + (n * esz) // 4]
        if dt != F32:
            a = a.bitcast(dt)
        if len(shape) == 2:
            a = a.rearrange("p (a b) -> p a b", a=shape[0])
        elif len(shape) == 3:
            a = a.rearrange("p (a b c) -> p a b c", a=shape[0], b=shape[1])
        return a

    state = {"bank": 0, "w": 0}
    bank_free = [None] * 8
    wfree = [None] * NW

    def next_bank():
        b = state["bank"]; state["bank"] = (b + 1) % 8
        return b

    def ACT(fn, deps=()):
        return P.emit("act", fn, deps, sig="act", serial=True)

    def DVE(fn, deps=()):
        return P.emit("dve", fn, deps, sig="dve", serial=True)

    def DMA(eng, key, fn, deps=()):
        return P.emit(eng, fn, deps, sig=key, inc=16)

    wflat = [w_[:].rearrange("p k c -> p (k c)") for w_ in wring]

    def load_w(tile_ap, pr, nk, ncols):
        slot = state["w"] % NW; state["w"] += 1
        key = f"w{slot}"
        dst = wflat[slot][0:pr, 0:nk * ncols]
        tok = DMA("pool", key, lambda e, d=dst, s=tile_ap: e.dma_start(out=d, in_=s), deps=[wfree[slot]])
        wv = dst.rearrange("p (k c) -> p k c", c=ncols)
        return slot, tok, wv

    def mm_group(out_ap, pairs, deps, bank, first=True, last=True, wslot=None):
        n = len(pairs)
        tok = None
        for i, (l, r) in enumerate(pairs):
            st_ = first and i == 0
            sp_ = last and i == n - 1
            dd = []
            if i == 0:
                dd = list(deps)
                if first:
                    dd.append(bank_free[bank])
            sig = "pe" if (i == n - 1) else None
            tok = P.emit("pe", lambda e, o=out_ap, l=l, r=r, a=st_, b=sp_: e.matmul(o, lhsT=l, rhs=r, start=a, stop=b),
                         deps=dd, sig=sig)
        if wslot is not None:
            wfree[wslot] = tok
        return tok

    def proj_fm(tile_ap, nk, ncols, rhs_fn, deps, pr=128):
        slot, wt, wv = load_w(tile_ap, pr, nk, ncols)
        b = next_bank()
        out = ps[b][0:ncols, 0:T]
        pairs = [(wv[:, k, :], rhs_fn(k)) for k in range(nk)]
        tok = mm_group(out, pairs, [wt] + list(deps), b, wslot=slot)
        return b, out, tok

    def DMA0(eng, key, fn):
        return P0.emit(eng, fn, (), sig=key, inc=16)
    DMA0("sp", "cst", lambda e: e.dma_start(out=c32[:], in_=c32_d))
    DMA0("sp", "cst", lambda e: e.dma_start(out=lnp[:].rearrange("p a k -> p (a k)"), in_=lnp_d))
    DMA0("sp", "cst", lambda e: e.dma_start(out=flg[:], in_=flg_d))
    t_sink = DMA0("sp", "cst", lambda e: e.dma_start(out=sink_t[:], in_=sink_d))
    t_cb = DMA0("pool", "cstp", lambda e: e.dma_start(out=cb16[:], in_=cb_d))
    P0.emit("dve", lambda e: e.memset(carry[:].rearrange("p a b c -> p (a b c)"), 0.0), (), sig="pdve")
    P0.emit("dve", lambda e: e.memset(kT[:].rearrange("p a b -> p (a b)"), 0.0), (), sig="pdve")
    t_ms = P0.emit("dve", lambda e: e.memset(Vt[:].rearrange("p a b -> p (a b)"), 0.0), (), sig="pdve")
    for e_ in ("pe", "act", "dve"):
        P.pending[e_] += [t_sink, t_cb, t_ms]

    hflat = h[:].rearrange("p k t -> p (k t)")
    hbflat = hb[:].rearrange("p k t -> p (k t)")

    def layer_norm(l, s, tok_y):
        P.barrier()
        ysq = view(0, [KC, T], F32)
        mean_sb = S[:, KC * T: KC * T + T]
        t1 = S[:, KC * T + T: KC * T + 2 * T]
        rstd = S[:, KC * T + 2 * T: KC * T + 3 * T]
        nmr = S[:, KC * T + 3 * T: KC * T + 4 * T]
        ta = ACT(lambda e: e.activation(out=ysq.rearrange("p k t -> p (k t)"), in_=hflat, func=AF.Square), deps=[tok_y])
        bm = next_bank(); bq = next_bank()
        mean_ps = ps[bm][:, 0:T]; msq_ps = ps[bq][:, 0:T]
        tm = mm_group(mean_ps, [(ones32, h[:, k, :]) for k in range(KC)], [tok_y], bm)
        tq = mm_group(msq_ps, [(ones32, ysq[:, k, :]) for k in range(KC)], [ta], bq)
        ta2 = ACT(lambda e: e.activation(out=mean_sb, in_=mean_ps, func=AF.Copy), deps=[tm])
        DVE(lambda e: e.tensor_tensor(out=t1, in0=mean_sb, in1=mean_sb, op=ALU.mult), deps=[ta2])
        DVE(lambda e: e.tensor_tensor(out=t1, in0=msq_ps, in1=t1, op=ALU.subtract), deps=[tq])
        tvar = DVE(lambda e: e.tensor_scalar(out=t1, in0=t1, scalar1=EPS, scalar2=None, op0=ALU.add))
        tsq = ACT(lambda e: e.activation(out=t1, in_=t1, func=AF.Sqrt), deps=[tvar])
        DVE(lambda e: e.reciprocal(out=rstd, in_=t1), deps=[tsq])
        td = DVE(lambda e: e.scalar_tensor_tensor(out=nmr, in0=mean_sb, scalar=-1.0, in1=rstd, op0=ALU.mult, op1=ALU.mult))
        bank_free[bm] = ta2; bank_free[bq] = td
        gi = (l * 2 + s) * 2
        rb = rstd.unsqueeze(1).to_broadcast([128, KC, T])
        nb_ = nmr.unsqueeze(1).to_broadcast([128, KC, T])
        gb_ = lnp[:, gi, :].unsqueeze(2).to_broadcast([128, KC, T])
        bb_ = lnp[:, gi + 1, :].unsqueeze(2).to_broadcast([128, KC, T])
        DVE(lambda e: e.tensor_tensor(out=h[:], in0=h[:], in1=rb, op=ALU.mult))
        DVE(lambda e: e.tensor_tensor(out=h[:], in0=h[:], in1=nb_, op=ALU.add))
        DVE(lambda e: e.tensor_tensor(out=h[:], in0=h[:], in1=gb_, op=ALU.mult))
        th = DVE(lambda e: e.tensor_tensor(out=h[:], in0=h[:], in1=bb_, op=ALU.add))
        thb = ACT(lambda e: e.activation(out=hbflat, in_=hflat, func=AF.Copy), deps=[th])
        return th, thb

    def resid_first(m, psum_ap, tok, bank):
        t = DVE(lambda e: e.scalar_tensor_tensor(out=h[:, m, :], in0=h[:, m, :], scalar=ALPHA, in1=psum_ap,
                                                 op0=ALU.mult, op1=ALU.add), deps=[tok])
        bank_free[bank] = t
        return t

    def resid_add(m, psum_ap, tok, bank):
        t = DVE(lambda e: e.tensor_tensor(out=h[:, m, :], in0=h[:, m, :], in1=psum_ap, op=ALU.add), deps=[tok])
        bank_free[bank] = t
        return t

    def gmlp(l, tok_hb):
        P.barrier()
        o = 0
        uT = view(o, [G, T], BF16); o += G * T * 2
        v32 = S[:, o // 4: o // 4 + DI]; o += DI * 4
        vbc = S[:, o // 4: o // 4 + DI // 2].bitcast(BF16); o += DI * 2
        WsT = view(o, [G, 128], BF16); o += G * 128 * 2
        HD = DI // 2
        gt = S[:, o // 4: o // 4 + HD]; o += HD * 4
        bt = S[:, o // 4: o // 4 + HD]; o += HD * 4
        bsb = S[:, o // 4: o // 4 + G * 128]; o += G * 128 * 4
        assert o <= o_tmp
        bar = [P.last["pe"], P.last["act"], P.last["dve"]]
        tws = DMA("pool", "sgu", lambda e: e.dma_start(out=WsT.rearrange("p g t -> p (g t)"),
                                                      in_=sguw_d[l * 128:(l + 1) * 128, :]), deps=bar)
        tbs = DMA("sp", "sgub", lambda e: e.dma_start(out=bsb, in_=sgub_d[l * 128:(l + 1) * 128, :]), deps=bar)
        cm = causal.unsqueeze(1).to_broadcast([128, G, 128])
        twm = DVE(lambda e: e.tensor_tensor(out=WsT, in0=WsT, in1=cm, op=ALU.mult), deps=[tws])
        for m in range(G):
            b, out, tok = proj_fm(w_in_a[(l * 2 * G + m) * 128:(l * 2 * G + m + 1) * 128, :], KC, 128, lambda k: hb[:, k, :], [tok_hb])
            ta = ACT(lambda e, m=m, out=out: e.activation(out=uT[:, m, :], in_=out, func=AF.Gelu), deps=[tok])
            bank_free[b] = ta
        tgb_use = None
        t_u_done = ta
        tvb = None
        for ch in range(TC):
            tv = None
            for nb4 in range(0, G, 4):
                b = next_bank()
                ng = min(4, G - nb4)
                tok = None
                for q_ in range(ng):
                    nbk = nb4 + q_
                    tix = l * 2 * G + G + nbk
                    slot, wt, wv = load_w(w_in_a[tix * 128:(tix + 1) * 128, :], 128, KC, 128)
                    pairs = [(hb[:, k, ch * 128:(ch + 1) * 128], wv[:, k, :]) for k in range(KC)]
                    dd = [wt, tok_hb]
                    if q_ > 0:
                        tok = mm_group(ps[b][:, q_ * 128:(q_ + 1) * 128], pairs, dd, b, first=True, wslot=slot)
                    else:
                        tok = mm_group(ps[b][:, 0:128], pairs, dd, b, first=True, wslot=slot)
                tv = ACT(lambda e, b=b, nb4=nb4, ng=ng: e.activation(out=v32[:, nb4 * 128:(nb4 + ng) * 128],
                                                                   in_=ps[b][:, 0:ng * 128], func=AF.Gelu),
                         deps=[tok, tgb_use])
                bank_free[b] = tv
            nchk = DI // 512 if DI >= 512 else 1
            cw = DI // nchk
            stats = st[:, 0:nchk * 6].rearrange("p (a b) -> p a b", a=nchk)
            for i in range(nchk):
                DVE(lambda e, i=i: e.bn_stats(out=stats[:, i, :], in_=v32[:, i * cw:(i + 1) * cw]), deps=[tv])
            mv = st[:, 48:50]
            DVE(lambda e: e.bn_aggr(out=mv, in_=stats))
            rs = st[:, 50:51]; nm = st[:, 51:52]
            tvar = DVE(lambda e: e.tensor_scalar(out=rs, in0=mv[:, 1:2], scalar1=EPS, scalar2=None, op0=ALU.add))
            tsq = ACT(lambda e: e.activation(out=rs, in_=rs, func=AF.Sqrt), deps=[tvar])
            DVE(lambda e: e.reciprocal(out=rs, in_=rs), deps=[tsq])
            tnm = DVE(lambda e: e.scalar_tensor_tensor(out=nm, in0=mv[:, 0:1], scalar=-1.0, in1=rs, op0=ALU.mult, op1=ALU.mult))
            tn = ACT(lambda e: e.activation(out=v32, in_=v32, func=AF.Identity, bias=nm, scale=rs), deps=[tnm])
            for hf in range(2):
                tg = DMA("sp", "vg", lambda e, hf=hf: e.dma_start(out=gt, in_=vgb_d[(l * 2) * 128:(l * 2 + 1) * 128, hf * HD:(hf + 1) * HD]),
                         deps=[tvb, bar[2]])
                tb = DMA("sp", "vg", lambda e, hf=hf: e.dma_start(out=bt, in_=vgb_d[(l * 2 + 1) * 128:(l * 2 + 2) * 128, hf * HD:(hf + 1) * HD]))
                DVE(lambda e, hf=hf: e.tensor_tensor(out=v32[:, hf * HD:(hf + 1) * HD], in0=v32[:, hf * HD:(hf + 1) * HD], in1=gt, op=ALU.mult),
                    deps=[tn, tb])
                tvb = DVE(lambda e, hf=hf: e.tensor_tensor(out=vbc[:, hf * HD:(hf + 1) * HD], in0=v32[:, hf * HD:(hf + 1) * HD], in1=bt, op=ALU.add))
            tgb_use = tvb
            tgate = None
            for g4 in range(0, G, 4):
                ng = min(4, G - g4)
                b = next_bank()
                tok = None
                for q_ in range(ng):
                    g_ = g4 + q_
                    tok = mm_group(ps[b][:, q_ * 128:(q_ + 1) * 128], [(vbc[:, g_ * 128:(g_ + 1) * 128], WsT[:, g_, :])],
                                   [tvb, twm], b)
                tmp = view(o_tmp, [4, 128], F32)
                pv = ps[b][:, 0:ng * 128].rearrange("p (g t) -> p g t", g=ng)
                bv = bsb[:, g4 * 128:(g4 + ng) * 128].rearrange("p (g t) -> p g t", g=ng)
                DVE(lambda e, pv=pv, bv=bv, ng=ng: e.tensor_tensor(out=tmp[:, 0:ng, :], in0=pv, in1=bv, op=ALU.add), deps=[tok, tbs, t_u_done])
                tgate = DVE(lambda e, g4=g4, ng=ng, ch=ch: e.tensor_tensor(out=uT[:, g4:g4 + ng, ch * 128:(ch + 1) * 128],
                                                                         in0=uT[:, g4:g4 + ng, ch * 128:(ch + 1) * 128],
                                                                         in1=tmp[:, 0:ng, :], op=ALU.mult))
                bank_free[b] = tgate
            tgb_use = tgate if tgate is not None else tgb_use
        tl = None
        for m in range(KC):
            b, out, tok = proj_fm(w_out_a[(l * KC + m) * 128:(l * KC + m + 1) * 128, :], G, 128, lambda k: uT[:, k, :], [tgate])
            tl = resid_first(m, out, tok, b)
        return tl

    o_tmp = SWB - 4 * 128 * 4

    def kv_proj(ti, tok_hb):
        P.barrier()
        DVE(lambda e: e.tensor_scalar(out=kT[0:64, :, 0:128], in0=kT[0:64, :, T:T + 128], scalar1=flg[0:64, bass.ds(2 * ctx["i"], 1)],
                                      scalar2=None, op0=ALU.mult))
        DVE(lambda e: e.tensor_scalar(out=Vt[:, 0, :], in0=Vt[:, TC, :], scalar1=flg[:, bass.ds(2 * ctx["i"], 1)],
                                      scalar2=None, op0=ALU.mult))
        tk = None
        for kvh in range(NKV):
            b, out, tok = proj_fm(w_k[kvh * 128:(kvh + 1) * 128, :], KC, 64, lambda k: hb[:, k, :], [tok_hb])
            tk = ACT(lambda e, kvh=kvh, out=out: e.activation(out=kT[0:64, kvh, 128:128 + T], in_=out, func=AF.Copy),
                     deps=[tok, P.last["dve"]])
            bank_free[b] = tk
        for ch in range(TC):
            for nbk in range(NVB):
                slot, wt, wv = load_w(w_v[nbk * 128:(nbk + 1) * 128, :], 128, KC, VB)
                b = next_bank()
                pairs = [(hb[:, k, ch * 128:(ch + 1) * 128], wv[:, k, :]) for k in range(KC)]
                tok = mm_group(ps[b][:, 0:VB], pairs, [wt, tok_hb], b, wslot=slot)
                tk = ACT(lambda e, b=b, ch=ch, nbk=nbk: e.activation(out=Vt[:, 1 + ch, nbk * VB:(nbk + 1) * VB], in_=ps[b][:, 0:VB], func=AF.Copy),
                         deps=[tok, P.last["dve"]])
                bank_free[b] = tk
        return tk

    def swa(j, ti, tok_hb, tok_kv):
        P.barrier()
        o = 0
        qT = view(o, [NQ, T], BF16); o += NQ * T * 2
        oT = view(o, [NQ, T], BF16); o += NQ * T * 2
        NB_ = 2
        s32 = [S[:, o // 4 + i * 256: o // 4 + (i + 1) * 256] for i in range(NB_)]; o += NB_ * 1024
        pn = [S[:, o // 4 + i * 128: o // 4 + (i + 1) * 128].bitcast(BF16) for i in range(NB_)]; o += NB_ * 512
        pT = [S[:, o // 4 + i * 128: o // 4 + (i + 1) * 128].bitcast(BF16) for i in range(NB_)]; o += NB_ * 512
        assert o <= SWB
        for hh in range(NQ):
            b, out, tok = proj_fm(w_in_b[(j * NQ + hh) * 128:(j * NQ + hh + 1) * 128, :], KC, 64, lambda k: hb[:, k, :], [tok_hb])
            tq = ACT(lambda e, hh=hh, out=out: e.activation(out=qT[0:64, hh, :], in_=out, func=AF.Copy, scale=0.125), deps=[tok])
            bank_free[b] = tq
        pn_free = [None] * NB_; pT_free = [None] * NB_
        tos = [None]

        def stage1(it):
            hh, blk = it // TC, it % TC
            kvh = hh // 8
            sk = sink_t[:, j * NQ + hh: j * NQ + hh + 1]
            i2 = it % NB_
            so = 8 * i2
            mx = st[:, so:so + 1]; ngm = st[:, so + 1:so + 2]; rsum = st[:, so + 2:so + 3]
            esk = st[:, so + 3:so + 4]
            mask = mask0[:] if blk == 0 else maskA
            b = next_bank()
            sps = ps[b][:, 0:256]
            tok = mm_group(sps, [(qT[0:64, hh, blk * 128:(blk + 1) * 128], kT[0:64, kvh, blk * 128:blk * 128 + 256])],
                           [tq, tok_kv], b)
            t1_ = DVE(lambda e: e.tensor_tensor(out=s32[i2], in0=sps, in1=mask, op=ALU.add), deps=[tok])
            bank_free[b] = t1_
            DVE(lambda e: e.tensor_reduce(out=mx, in_=s32[i2], axis=AX.X, op=ALU.max))
            t2_ = DVE(lambda e: e.tensor_scalar(out=ngm, in0=mx, scalar1=sk, scalar2=-1.0, op0=ALU.max, op1=ALU.mult))
            ACT(lambda e: e.activation(out=s32[i2], in_=s32[i2], func=AF.Exp, bias=ngm, scale=1.0, accum_out=rsum), deps=[t2_])
            t3_ = ACT(lambda e: e.activation(out=esk, in_=sk, func=AF.Exp, bias=ngm, scale=1.0))
            return t3_

        def stage2(it, t3_):
            hh, blk = it // TC, it % TC
            kvh = hh // 8
            i2 = it % NB_
            so = 8 * i2
            rsum = st[:, so + 2:so + 3]; esk = st[:, so + 3:so + 4]; den = st[:, so + 4:so + 5]; rinv = st[:, so + 5:so + 6]
            DVE(lambda e: e.tensor_tensor(out=den, in0=rsum, in1=esk, op=ALU.add), deps=[t3_])
            DVE(lambda e: e.reciprocal(out=rinv, in_=den))
            t4_ = DVE(lambda e: e.tensor_scalar(out=pn[i2], in0=s32[i2], scalar1=rinv, scalar2=None, op0=ALU.mult),
                      deps=[pn_free[i2]])
            b2 = next_bank()
            ptp = ps[b2][:].bitcast(BF16)
            P.emit("pe", lambda e: e.transpose(ptp[:, 0:128], pn[i2][:, 0:128], ident), deps=[t4_, bank_free[b2]])
            t5_ = P.emit("pe", lambda e: e.transpose(ptp[:, 128:256], pn[i2][:, 128:256], ident), sig="pe")
            pn_free[i2] = t5_
            t6_ = ACT(lambda e: e.activation(out=pT[i2], in_=ptp[:, 0:256], func=AF.Copy), deps=[t5_, pT_free[i2]])
            bank_free[b2] = t6_
            b3 = next_bank()
            ops_ = ps[b3][0:64, 0:128]
            t7_ = mm_group(ops_, [(Vt[:, blk, kvh * 64:(kvh + 1) * 64], pT[i2][:, 0:128]),
                                  (Vt[:, blk + 1, kvh * 64:(kvh + 1) * 64], pT[i2][:, 128:256])], [t6_, tok_kv], b3)
            pT_free[i2] = t7_
            to_ = ACT(lambda e: e.activation(out=oT[0:64, hh, blk * 128:(blk + 1) * 128], in_=ops_, func=AF.Copy), deps=[t7_])
            bank_free[b3] = to_
            tos[0] = to_

        NIT = NQ * TC
        prev = None
        for it in range(NIT):
            t3_ = stage1(it)
            if prev is not None:
                stage2(it - 1, prev)
            prev = t3_
        stage2(NIT - 1, prev)
        to = tos[0]
        tl = None
        for m in range(KC):
            b = next_bank()
            out = ps[b][:, 0:T]
            tok = None
            nparts = NPARTS
            for pi in range(nparts):
                tix = (j * KC + m) * NPARTS + pi
                slot, wt, wv = load_w(w_out_b[tix * 64:(tix + 1) * 64, :], 64, HB, 128)
                pairs = [(wv[:, k, :], oT[0:64, pi * HB + k, :]) for k in range(HB)]
                tok = mm_group(out, pairs, [wt, to], b, first=(pi == 0), last=(pi == nparts - 1), wslot=slot)
            tl = resid_first(m, out, tok, b)
        return tl

    def ffn(l, ti, tok_hb):
        P.barrier()
        GS = max(gsz)
        o = 0
        hid = view(o, [GS, T], BF16); o += GS * T * 2
        NZ = 2
        zb = [[S[:, o // 4 + (2 * i + w) * (T + 2): o // 4 + (2 * i + w + 1) * (T + 2)] for w in range(2)] for i in range(NZ)]
        o += NZ * 2 * (T + 2) * 4
        cg = [[S[:, o // 4 + (2 * i + w) * T: o // 4 + (2 * i + w + 1) * T] for w in range(2)] for i in range(NZ)]
        o += NZ * 2 * T * 4
        assert o <= SWB
        bar = [P.last["pe"], P.last["act"], P.last["dve"]]
        tcp = DMA("sp", "convp", lambda e: e.dma_start(out=convp[:].rearrange("p a j -> p (a j)"), in_=convp_d[l * 128:(l + 1) * 128, :]), deps=bar)
        it = 0
        zfree = [[None, None] for _ in range(NZ)]
        thid_free = None
        tl = None
        for gi, (j0, j1) in enumerate(groups):
            thid = None
            for j in range(j0, j1):
                i2 = it % NZ; it += 1
                tc_ = [None, None]
                for w in range(2):
                    jj = w * NJ + j
                    b, out, tok = proj_fm(w_up[(l * NJ2 + jj) * 128:(l * NJ2 + jj + 1) * 128, :], KC, 128, lambda k: hb[:, k, :], [tok_hb])
                    z = zb[i2][w]; cgt = cg[i2][w]
                    tz = ACT(lambda e, z=z, out=out: e.activation(out=z[:, 2:2 + T], in_=out, func=AF.Copy), deps=[tok, zfree[i2][w]])
                    bank_free[b] = tz
                    DVE(lambda e, z=z, jj=jj: e.tensor_copy(out=z[:, 0:2], in_=carry[:, l, jj, :]), deps=[zfree[i2][w]])
                    DVE(lambda e, z=z, jj=jj: e.tensor_copy(out=carry[:, l, jj, :], in_=z[:, T:T + 2]), deps=[tz])
                    DVE(lambda e, z=z, cgt=cgt, jj=jj: e.tensor_scalar(out=cgt, in0=z[:, 2:2 + T], scalar1=convp[:, 2, jj:jj + 1],
                                                                     scalar2=convp[:, 3, jj:jj + 1], op0=ALU.mult, op1=ALU.add), deps=[tcp])
                    DVE(lambda e, z=z, cgt=cgt, jj=jj: e.scalar_tensor_tensor(out=cgt, in0=z[:, 1:1 + T], scalar=convp[:, 1, jj:jj + 1], in1=cgt,
                                                                            op0=ALU.mult, op1=ALU.add))
                    tc_[w] = DVE(lambda e, z=z, cgt=cgt, jj=jj: e.scalar_tensor_tensor(out=cgt, in0=z[:, 0:T], scalar=convp[:, 0, jj:jj + 1], in1=cgt,
                                                                                     op0=ALU.mult, op1=ALU.add))
                    zfree[i2][w] = tc_[w]
                sg = cg[i2][0]; cu = cg[i2][1]
                tsg = ACT(lambda e, sg=sg: e.activation(out=sg, in_=sg, func=AF.Silu), deps=[tc_[0]])
                thid = DVE(lambda e, sg=sg, cu=cu, jl=j - j0: e.tensor_tensor(out=hid[:, jl, :], in0=sg, in1=cu, op=ALU.mult),
                           deps=[tsg, thid_free])
                zfree[i2][0] = thid; zfree[i2][1] = thid
            nk = j1 - j0
            for m in range(KC):
                b, out, tok = proj_fm(w_down[(l * KC + m) * 128:(l * KC + m + 1) * 128, j0 * 128:j1 * 128], nk, 128, lambda k: hid[:, k, :], [thid])
                tl = resid_first(m, out, tok, b) if gi == 0 else resid_add(m, out, tok, b)
                thid_free = tok
        return tl

    ti = None
    tx = DMA("sp", "x", lambda e: e.dma_start(out=hflat, in_=xT[bass.ds(ctx["i"] * 128, 128), :]))
    tok_hb = ACT(lambda e: e.activation(out=hbflat, in_=hflat, func=AF.Copy), deps=[tx])
    DVE(lambda e: e.tensor_scalar(out=carry[:].rearrange("p a b c -> p (a b c)"), in0=carry[:].rearrange("p a b c -> p (a b c)"),
                                  scalar1=flg[:, bass.ds(2 * ctx["i"], 1)], scalar2=None, op0=ALU.mult))
    DVE(lambda e: e.scalar_tensor_tensor(out=mask0[:], in0=maskX, scalar=flg[:, bass.ds(2 * ctx["i"] + 1, 1)], in1=maskA,
                                         op0=ALU.mult, op1=ALU.add))
    tok_kv = None
    import os
    _dbg = int(os.environ.get("KDBG", "99"))
    th = P.last["dve"]
    for l in range(min(DEPTH, _dbg)):
        if l < N_A:
            ty = gmlp(l, tok_hb)
        else:
            if l == N_A:
                tok_kv = kv_proj(ti, tok_hb)
            ty = swa(l - N_A, ti, tok_hb, tok_kv)
        th, tok_hb = layer_norm(l, 0, ty)
        ty = ffn(l, ti, tok_hb)
        th, tok_hb = layer_norm(l, 1, ty)
    t_store = DMA("sp", "y", lambda e: e.dma_start(out=yT[bass.ds(ctx["i"] * 128, 128), :], in_=hflat), deps=[th])
    fin = [P.last["pe"], P.last["act"], P.last["dve"], t_store]
    for e_ in ENGS:
        P.emit(e_, None, deps=fin)

    keys = set(P.cnt) | set(P0.cnt)
    sems = {k: es.enter_context(nc.semaphore(f"s_{k}")) for k in sorted(keys)}
    semB1 = es.enter_context(nc.semaphore("s_bar1"))
    semB2 = es.enter_context(nc.semaphore("s_bar2"))
    pool_keys = set()
    for wl, fn, sig, inc in P.ops["pool"]:
        if sig is not None:
            pool_keys.add(sig)
    clear_keys = sorted(set(P.cnt) - pool_keys)
    ENG_OBJ = {"pe": nc.tensor, "act": nc.scalar, "dve": nc.vector, "pool": nc.gpsimd, "sp": nc.sync}
    regs = {}
    for name in ENGS:
        for wl, fn, sig, inc in P.ops[name]:
            for k, v in wl:
                if k in pool_keys and (name, k) not in regs:
                    regs[(name, k)] = ENG_OBJ[name].alloc_register(f"r_{name}_{k}")

    def replay(prog, name, e, body):
        pos = {}
        for wl, fn, sig, inc in prog.ops[name]:
            for k, v in wl:
                if body and k in pool_keys:
                    r = regs[(name, k)]
                    e.reg_alu(r, r, v - pos.get(k, 0), ALU.add)
                    pos[k] = v
                    e.wait_ge(sems[k], r)
                else:
                    e.wait_ge(sems[k], v)
            if fn is None:
                continue
            ins = fn(e)
            if sig is not None:
                ins.then_inc(sems[sig], inc)
        if body:
            for (n_, k), r in regs.items():
                if n_ == name and P.cnt[k] - pos.get(k, 0) != 0:
                    e.reg_alu(r, r, P.cnt[k] - pos.get(k, 0), ALU.add)

    def loop_tail(name, e, i):
        e.sem_inc(semB1, 1)
        if name == "sp":
            e.wait_ge(semB1, (i + 1) * 5)
            for k in clear_keys:
                e.sem_clear(sems[k])
            e.sem_inc(semB2, 1)
        e.wait_ge(semB2, i + 1)

    def sections(prog, i):
        block = es2.enter_context(nc.Block())

        def mk(name):
            def _(e):
                if i is None:
                    for (n_, k), r in regs.items():
                        if n_ == name:
                            e.reg_mov(r, 0)
                replay(prog, name, e, i is not None)
                if i is not None:
                    loop_tail(name, e, i)
            return _

        block.tensor(mk("pe"))
        block.scalar(mk("act"))
        block.vector(mk("dve"))
        block.gpsimd(mk("pool"))
        block.sync(mk("sp"))

    with ExitStack() as es2:
        sections(P0, None)
    with nc.Fori(0, NTILES) as it_:
        ctx["i"] = it_
        with ExitStack() as es2:
            sections(P, it_)

    es.close()
    ninstr = {k: len(v) for k, v in P.ops.items()}
    return nc, ninstr


def host_prep(cfg, inputs, core):
    c = cfg
    D, DI, DFF, SEQ, NBC = c.D, c.DI, c.DFF, c.SEQ, c.NBC
    KC, G, NJ, NJ2, NQ = c.KC, c.G, c.NJ, c.NJ2, c.NQ
    f = lambda a: np.ascontiguousarray(np.asarray(a, dtype=np.float32))
    x = np.asarray(inputs["x"])[core * NBC:(core + 1) * NBC]
    m = {}
    NT, T = c.NT, c.T
    xt = x.reshape(NBC, NT, T, KC, 128)
    m["xT"] = f(np.transpose(xt, (0, 1, 4, 3, 2)).reshape(NBC * NT * 128, KC * T))
    fl = np.zeros((128, 2 * NBC * NT), np.float32)
    for i in range(NBC * NT):
        first = (i % NT == 0)
        fl[:, 2 * i] = 0.0 if first else 1.0
        fl[:, 2 * i + 1] = 1.0 if first else 0.0
    m["flg"] = fl
    return m


def shared_prep(cfg, inputs):
    c = cfg
    D, DI, DFF, SEQ = c.D, c.DI, c.DFF, c.SEQ
    KC, G, NJ, NJ2, NQ = c.KC, c.G, c.NJ, c.NJ2, c.NQ
    DEPTH, N_A, N_B = c.DEPTH, c.N_A, c.N_B
    f = lambda a: np.ascontiguousarray(np.asarray(a, dtype=np.float32))
    m = {}
    NKV, KVD = c.NKV, c.KVD
    VB = min(128, KVD); NVB = KVD // VB
    HB = 32 if NQ >= 32 else NQ; NPARTS = NQ // HB
    tr = lambda a, shape, perm, out: np.ascontiguousarray(np.transpose(np.asarray(a, dtype=np.float32).reshape(shape), perm)).reshape(out)
    m["w_in_a"] = tr(inputs["mix_in_a"], (N_A, KC, 128, 2 * G, 128), (0, 3, 2, 1, 4), (N_A * 2 * G * 128, KC * 128))
    m["w_out_a"] = tr(inputs["mix_out_a"], (N_A, G, 128, KC, 128), (0, 3, 2, 1, 4), (N_A * KC * 128, G * 128))
    wkv = np.asarray(inputs["w_kv"], dtype=np.float32)
    m["w_k"] = tr(wkv[:, :KVD], (KC, 128, NKV, 64), (2, 1, 0, 3), (NKV * 128, KC * 64))
    m["w_v"] = tr(wkv[:, KVD:], (KC, 128, NVB, VB), (2, 1, 0, 3), (NVB * 128, KC * VB))
    m["w_in_b"] = tr(inputs["mix_in_b"], (N_B, KC, 128, NQ, 64), (0, 3, 2, 1, 4), (N_B * NQ * 128, KC * 64))
    m["w_out_b"] = tr(inputs["mix_out_b"], (N_B, NPARTS, HB, 64, KC, 128), (0, 4, 1, 3, 2, 5), (N_B * KC * NPARTS * 64, HB * 128))
    m["w_up"] = tr(inputs["ffn_up"], (DEPTH, KC, 128, NJ2, 128), (0, 3, 2, 1, 4), (DEPTH * NJ2 * 128, KC * 128))
    m["w_down"] = tr(inputs["ffn_down"], (DEPTH, NJ, 128, KC, 128), (0, 3, 2, 1, 4), (DEPTH * KC * 128, NJ * 128))
    lg = f(inputs["ln_g"]).reshape(DEPTH, 2, KC, 128)
    lb = f(inputs["ln_b"]).reshape(DEPTH, 2, KC, 128)
    lnp = np.stack([lg, lb], axis=2)
    m["lnp"] = f(np.transpose(lnp, (4, 0, 1, 2, 3)).reshape(128, DEPTH * 4 * KC))
    cw = f(inputs["ffn_conv_w"]).reshape(DEPTH, 3, NJ2, 128)
    cbb = f(inputs["ffn_conv_b"]).reshape(DEPTH, 1, NJ2, 128)
    cp = np.concatenate([cw, cbb], axis=1)
    m["convp"] = f(np.transpose(cp, (0, 3, 1, 2)).reshape(DEPTH * 128, 4 * NJ2))
    m["sinkb"] = f(np.broadcast_to(f(inputs["sinks"]).reshape(1, N_B * NQ), (128, N_B * NQ)))
    vg = f(inputs["norm_v_a_g"]); vb = f(inputs["norm_v_a_b"])
    vgb = np.stack([vg, vb], axis=1)
    m["vgb"] = f(np.broadcast_to(vgb[:, :, None, :], (N_A, 2, 128, DI)).reshape(N_A * 2 * 128, DI))
    sb_ = f(inputs["sgu_b"]).reshape(N_A, 1, G * 128)
    m["sgub"] = f(np.broadcast_to(sb_, (N_A, 128, G * 128)).reshape(N_A * 128, G * 128))
    sw = f(inputs["sgu_w"])
    m["sguwT"] = f(np.transpose(sw, (0, 3, 1, 2)).reshape(N_A * 128, G * 128))
    qi = np.arange(128)[:, None]; kj = np.arange(256)[None, :]
    band = (kj > qi) & (kj <= qi + 128)
    NEG = -30000.0
    cst = np.zeros((128, 640), np.float32)
    cst[:, 0:128] = 1.0 / D
    cst[:, 128:384] = np.where(band, 0.0, NEG)
    cst[:, 384:640] = np.where(np.broadcast_to(kj < 128, (128, 256)), NEG, 0.0)
    m["cst32"] = cst
    cb = np.zeros((128, 256), np.float32)
    cb[:, 0:128] = np.eye(128, dtype=np.float32)
    s_ = np.arange(128)[:, None]; t_ = np.arange(128)[None, :]
    cb[:, 128:256] = (t_ >= s_).astype(np.float32)
    m["cstb"] = cb
    return m


_FULL = dict(D=4096, DFF=11008, SEQ=4096, NBC=1, NCORES=4)


def run(cfg, inputs, trace=False):
    nc, ninstr = build(cfg)
    shared = shared_prep(cfg, inputs)
    in_maps = []
    for core in range(cfg.NCORES):
        m = dict(shared)
        m.update(host_prep(cfg, inputs, core))
        in_maps.append(m)
    res = run_bass_kernel_spmd(nc, in_maps, core_ids=list(range(cfg.NCORES)), trace=trace)
    outs = []
    for core in range(cfg.NCORES):
        y = np.asarray(res.results[core]["yT"]).reshape(cfg.NBC, cfg.NT, 128, cfg.KC, cfg.T)
        outs.append(np.transpose(y, (0, 1, 4, 3, 2)).reshape(cfg.NBC, cfg.SEQ, cfg.D))
    return np.ascontiguousarray(np.concatenate(outs, axis=0).astype(np.float32)), res


def kernel(**inputs):
    cfg = Cfg(**_FULL)
    out, _ = run(cfg, inputs)
    return out
```
